# Optimizing a Trainium2 kernel written in Bass

```python
import math
import jax, jax.numpy as jnp
from jax import lax
import numpy as np

D_MODEL = 1024
BATCH = 8
SEQ = 8192
DEPTH = 2

GRID_W = 64
CTX_LEN = 256
N_MIXERS = 2
N_MOD = 9
D_FF = 2816
D_RNN = D_MODEL
LRU_HEADS = 8
LRU_BLOCK = D_RNN // LRU_HEADS
CONV_W = 4
CONV_LEFT = 2
RG_C = 8.0
CHUNK = 128
D_SGU = 2 * D_MODEL
SGU_GROUPS = 8
SGU_GROUP_W = D_SGU // SGU_GROUPS
EPS = 1e-6
POS_BASE = 10000.0

kernel_name = "hybrid_rglru_chunk_sgu_diffusion_trunk"


def rmsnorm(x, g):
    xf = x.astype(jnp.float32)
    y = xf * lax.rsqrt(jnp.mean(xf * xf, axis=-1, keepdims=True) + EPS)
    return y.astype(x.dtype) * g


def layernorm(x, g, b):
    xf = x.astype(jnp.float32)
    mu = jnp.mean(xf, axis=-1, keepdims=True)
    var = jnp.mean(jnp.square(xf - mu), axis=-1, keepdims=True)
    return ((xf - mu) * lax.rsqrt(var + EPS)).astype(x.dtype) * g + b


def sincos_1d(pos, dim):
    half = dim // 2
    omega = 1.0 / (POS_BASE ** (jnp.arange(half, dtype=jnp.float32) / half))
    ang = pos.astype(jnp.float32)[:, None] * omega[None, :]
    return jnp.concatenate([jnp.sin(ang), jnp.cos(ang)], axis=-1)


def sincos_2d(n_tokens, dim):
    rows = n_tokens // GRID_W
    emb_r = sincos_1d(jnp.arange(rows), dim // 2)
    emb_c = sincos_1d(jnp.arange(GRID_W), dim // 2)
    pe = jnp.concatenate([jnp.broadcast_to(emb_r[:, None, :], (rows, GRID_W, dim // 2)),
                          jnp.broadcast_to(emb_c[None, :, :], (rows, GRID_W, dim // 2))], axis=-1)
    return pe.reshape(rows * GRID_W, dim)


def swiglu(h, w1, w3, w2):
    return (jax.nn.silu(h @ w1) * (h @ w3)) @ w2


def centred_dwconv(x, w, b):
    L = x.shape[1]
    xp = jnp.pad(x, ((0, 0), (CONV_LEFT, CONV_W - 1 - CONV_LEFT), (0, 0)))
    y = xp[:, 0:L] * w[0]
    for k in range(1, CONV_W):
        y = y + xp[:, k:k + L] * w[k]
    return y + b


def _lin_combine(left, right):
    a_l, b_l = left
    a_r, b_r = right
    return a_l * a_r, a_r * b_l + b_r


def rglru_scan(x, wa, ba, wi, bi, lam, h0, reverse):
    B_, L, R = x.shape
    xf = x.astype(jnp.float32)
    xh = xf.reshape(B_, L, LRU_HEADS, LRU_BLOCK)
    r = jax.nn.sigmoid(jnp.einsum('blhi,hij->blhj', xh, wa.astype(jnp.float32)).reshape(B_, L, R) + ba.astype(jnp.float32))
    ig = jax.nn.sigmoid(jnp.einsum('blhi,hij->blhj', xh, wi.astype(jnp.float32)).reshape(B_, L, R) + bi.astype(jnp.float32))
    log_a = -RG_C * r * jax.nn.softplus(-lam.astype(jnp.float32))
    a = jnp.exp(log_a)
    b = jnp.sqrt(-jnp.expm1(2.0 * log_a)) * (ig * xf)
    if reverse:
        a = jnp.flip(a, axis=1)
        b = jnp.flip(b, axis=1)
    b = b.at[:, 0].add(a[:, 0] * h0)
    _, hs = lax.associative_scan(_lin_combine, (a, b), axis=1)
    if reverse:
        hs = jnp.flip(hs, axis=1)
    return hs


def rglru_block(h, w_in, conv_w, conv_b, wa, ba, wi, bi, lam, w_out, h0_f, h0_b):
    z = h @ w_in
    gate_br = jax.nn.gelu(z[..., :D_RNN], approximate=True)
    xc = centred_dwconv(z[..., D_RNN:], conv_w, conv_b)
    hf = rglru_scan(xc, wa[0], ba[0], wi[0], bi[0], lam[0], h0_f, reverse=False)
    hb = rglru_scan(xc, wa[1], ba[1], wi[1], bi[1], lam[1], h0_b, reverse=True)
    y = ((hf + hb).astype(gate_br.dtype) * gate_br) @ w_out
    return y, hf[:, -1], hb[:, 0]


def chunk_sgu(h, w_in, b_in, ln_g, ln_b, ws, bs, w_out):
    B_, L, _ = h.shape
    z = jax.nn.gelu(h @ w_in + b_in, approximate=True)
    u = z[..., :D_SGU]
    v = layernorm(z[..., D_SGU:], ln_g, ln_b)
    v = v.reshape(B_, L // CHUNK, CHUNK, SGU_GROUPS, SGU_GROUP_W)
    s = jnp.einsum('bnpgc,gqp->bnqgc', v, ws) + jnp.transpose(bs)[:, :, None]
    return (u * s.reshape(B_, L, D_SGU)) @ w_out


def setup_inputs(seed: int = 0) -> dict:
    key = jax.random.key(seed)
    ks = iter(jax.random.split(key, 40))
    f32 = jnp.float32

    def nrm(shape, s):
        return jax.random.normal(next(ks), shape, f32) * s

    n_a = len([i for i in range(DEPTH) if i % N_MIXERS == 0])
    n_b = len([i for i in range(DEPTH) if i % N_MIXERS == 1])
    u = jax.random.uniform(next(ks), (n_a, 2, D_RNN), f32, minval=0.9, maxval=0.999)
    a0 = u ** (1.0 / RG_C)
    lam = jnp.log(a0) - jnp.log1p(-a0)
    return {
        "x": nrm((BATCH, SEQ, D_MODEL), 1.0),
        "c": nrm((BATCH, D_MODEL), 1.0),
        "ctx": nrm((BATCH, CTX_LEN, D_MODEL), 1.0),
        "c_ctx": nrm((D_MODEL,), 1.0),
        "ada_w": nrm((DEPTH, D_MODEL, N_MOD * D_MODEL), 0.5 * D_MODEL ** -0.5),
        "ada_b": nrm((DEPTH, N_MOD * D_MODEL), 0.01),
        "norm_pre": 1.0 + nrm((DEPTH, 3, D_MODEL), 0.02),
        "norm_post": 1.0 + nrm((DEPTH, 3, D_MODEL), 0.02),
        "ffn_w1": nrm((DEPTH, 2, D_MODEL, D_FF), D_MODEL ** -0.5),
        "ffn_w3": nrm((DEPTH, 2, D_MODEL, D_FF), D_MODEL ** -0.5),
        "ffn_w2": nrm((DEPTH, 2, D_FF, D_MODEL), D_FF ** -0.5),
        "lru_w_in": nrm((n_a, D_MODEL, 2 * D_RNN), D_MODEL ** -0.5),
        "lru_conv_w": nrm((n_a, CONV_W, D_RNN), CONV_W ** -0.5),
        "lru_conv_b": nrm((n_a, D_RNN), 0.01),
        "lru_wa": nrm((n_a, 2, LRU_HEADS, LRU_BLOCK, LRU_BLOCK), LRU_BLOCK ** -0.5),
        "lru_ba": nrm((n_a, 2, D_RNN), 0.01),
        "lru_wi": nrm((n_a, 2, LRU_HEADS, LRU_BLOCK, LRU_BLOCK), LRU_BLOCK ** -0.5),
        "lru_bi": nrm((n_a, 2, D_RNN), 0.01),
        "lru_lambda": lam,
        "lru_w_out": nrm((n_a, D_RNN, D_MODEL), D_RNN ** -0.5),
        "sgu_w_in": nrm((n_b, D_MODEL, 2 * D_SGU), D_MODEL ** -0.5),
        "sgu_b_in": nrm((n_b, 2 * D_SGU), 0.01),
        "sgu_ln_g": 1.0 + nrm((n_b, D_SGU), 0.02),
        "sgu_ln_b": nrm((n_b, D_SGU), 0.01),
        "sgu_ws": nrm((n_b, SGU_GROUPS, CHUNK, CHUNK), CHUNK ** -0.5),
        "sgu_bs": 1.0 + nrm((n_b, SGU_GROUPS, CHUNK), 0.02),
        "sgu_w_out": nrm((n_b, D_SGU, D_MODEL), D_SGU ** -0.5),
    }


def reference(x, c, ctx, c_ctx, ada_w, ada_b, norm_pre, norm_post, ffn_w1, ffn_w3, ffn_w2,
              lru_w_in, lru_conv_w, lru_conv_b, lru_wa, lru_ba, lru_wi, lru_bi, lru_lambda, lru_w_out,
              sgu_w_in, sgu_b_in, sgu_ln_g, sgu_ln_b, sgu_ws, sgu_bs, sgu_w_out):
    n_lat = x.shape[1]
    x_lat = x + sincos_2d(n_lat, D_MODEL).astype(x.dtype)
    x_ctx = ctx

    def pre(xs, i, k, m):
        return rmsnorm(xs, norm_pre[i, k]) * (1.0 + m[3 * k + 1]) + m[3 * k]

    def post(xs, y, i, k, m, w):
        return xs + w * m[3 * k + 2] * rmsnorm(y, norm_post[i, k])

    def ffn_sub(xs, i, k, j, m):
        y = swiglu(pre(xs, i, k, m), ffn_w1[i, j], ffn_w3[i, j], ffn_w2[i, j])
        return post(xs, y, i, k, m, 0.5)

    for i in range(DEPTH):
        mixer = i % N_MIXERS
        mi = i // N_MIXERS
        need_ctx = any(j % N_MIXERS == 0 for j in range(i, DEPTH))
        ml = (jax.nn.silu(c) @ ada_w[i] + ada_b[i]).reshape(c.shape[0], N_MOD, D_MODEL)
        m_lat = [ml[:, k, None, :] for k in range(N_MOD)]
        mc = (jax.nn.silu(c_ctx) @ ada_w[i] + ada_b[i]).reshape(N_MOD, D_MODEL)
        m_ctx = [mc[k] for k in range(N_MOD)]

        x_lat = ffn_sub(x_lat, i, 0, 0, m_lat)
        if need_ctx:
            x_ctx = ffn_sub(x_ctx, i, 0, 0, m_ctx)

        if mixer == 0:
            lru_args = (lru_w_in[mi], lru_conv_w[mi], lru_conv_b[mi], lru_wa[mi], lru_ba[mi],
                        lru_wi[mi], lru_bi[mi], lru_lambda[mi], lru_w_out[mi])
            zeros = jnp.zeros((x_ctx.shape[0], D_RNN), jnp.float32)
            yc, hc_f, hc_b = rglru_block(pre(x_ctx, i, 1, m_ctx), *lru_args, zeros, zeros)
            yl, _, _ = rglru_block(pre(x_lat, i, 1, m_lat), *lru_args, hc_f, hc_b)
            x_ctx = post(x_ctx, yc, i, 1, m_ctx, 1.0)
            x_lat = post(x_lat, yl, i, 1, m_lat, 1.0)
        else:
            sgu_args = (sgu_w_in[mi], sgu_b_in[mi], sgu_ln_g[mi], sgu_ln_b[mi], sgu_ws[mi], sgu_bs[mi], sgu_w_out[mi])
            x_lat = post(x_lat, chunk_sgu(pre(x_lat, i, 1, m_lat), *sgu_args), i, 1, m_lat, 1.0)
            if need_ctx:
                x_ctx = post(x_ctx, chunk_sgu(pre(x_ctx, i, 1, m_ctx), *sgu_args), i, 1, m_ctx, 1.0)

        x_lat = ffn_sub(x_lat, i, 2, 1, m_lat)
        if need_ctx and any(j % N_MIXERS == 0 for j in range(i + 1, DEPTH)):
            x_ctx = ffn_sub(x_ctx, i, 2, 1, m_ctx)
    return x_lat
```

```python
import contextlib
import math
import numpy as np
import concourse.bass as bass
import concourse.mybir as mybir
from concourse.bass_utils import run_bass_kernel_spmd

F32 = mybir.dt.float32
BF16 = mybir.dt.bfloat16
I32 = mybir.dt.int32
AF = mybir.ActivationFunctionType
ALU = mybir.AluOpType

P = 128
D = 1024
KC = 8
DFF = 2816
FC = 22
T = 256
CTX = 256
SEQ = 8192
NCORES = 8
EPS = 1e-6
RG_C = 8.0
GRID_W = 64

ENGS = ["pe", "act", "dve", "pool", "sp"]
_DBG = {}

_ROWS = {}
_r = 0
for _name, _n in [("ada_b", 144), ("npre", 48), ("npost", 48), ("convw", 32), ("convb", 8),
                  ("ba", 16), ("bi", 16), ("lam", 16), ("sbin", 16), ("lng", 16), ("lnb", 16),
                  ("c", 8), ("cctx", 8)]:
    _ROWS[_name] = _r
    _r += _n
NROWS = _r
NROWS_PAD = 512


class Op:
    __slots__ = ("fn", "deps", "dma", "tok")

    def __init__(self, fn, deps, dma, tok):
        self.fn, self.deps, self.dma, self.tok = fn, deps, dma, tok


class Sched:
    def __init__(self):
        self.ops = {e: [] for e in ENGS}
        self.lastw = {}
        self.readers = {}
        self.dma_cum = {}

    def add(self, eng, fn, r=(), w=(), dma=None, ndma=1):
        idx = len(self.ops[eng])
        strong = set()
        weak = set()
        for k in r:
            t = self.lastw.get(k)
            if t is not None:
                strong.add(t)
        for k in w:
            t = self.lastw.get(k)
            if t is not None:
                strong.add(t)
            weak.update(self.readers.get(k, ()))
        if dma is not None:
            cum = self.dma_cum.get(dma, 0) + 16 * ndma
            self.dma_cum[dma] = cum
            tok = ("dma", dma, cum)
        else:
            tok = ("eng", eng, idx)
        deps = set()
        for d in strong:
            if d[0] == "eng" and d[1] == eng and eng == "pe":
                continue
            deps.add(d)
        for d in weak:
            if d[0] == "eng" and d[1] == eng and eng == "pe":
                continue
            deps.add(d)
        for k in r:
            self.readers.setdefault(k, []).append(tok)
        for k in w:
            self.lastw[k] = tok
            self.readers[k] = []
        self.ops[eng].append(Op(fn, deps, dma, tok))
        return tok

    def barrier(self):
        toks = set()
        for e in ENGS:
            if self.ops[e]:
                for op in reversed(self.ops[e]):
                    if op.tok is not None and op.tok[0] == "eng":
                        toks.add(op.tok)
                        break
        for name, cum in self.dma_cum.items():
            toks.add(("dma", name, cum))
        for e in ENGS:
            deps = set(t for t in toks if not (t[0] == "eng" and t[1] == e))
            self.ops[e].append(Op(None, deps, None, None))
        self.lastw = {}
        self.readers = {}

    def emit(self, nc, stack):
        needed = set()
        for e in ENGS:
            for op in self.ops[e]:
                for d in op.deps:
                    if d[0] == "eng":
                        needed.add((d[1], d[2]))
        count = {}
        for e in ENGS:
            c = 0
            for i, op in enumerate(self.ops[e]):
                if (e, i) in needed:
                    c += 1
                    count[(e, i)] = c
        esem = {e: stack.enter_context(nc.semaphore("s_" + e)) for e in ENGS}
        dsem = {n: stack.enter_context(nc.semaphore("d_" + n)) for n in self.dma_cum}
        final = dict(self.dma_cum)

        def run(e, eng):
            waited = {}
            for i, op in enumerate(self.ops[e]):
                reqs = {}
                for d in op.deps:
                    if d[0] == "eng":
                        s, v = esem[d[1]], count[(d[1], d[2])]
                        key = ("e", d[1])
                    else:
                        s, v = dsem[d[1]], d[2]
                        key = ("d", d[1])
                    if waited.get(key, 0) >= v:
                        continue
                    if key not in reqs or reqs[key][1] < v:
                        reqs[key] = (s, v)
                for key, (s, v) in reqs.items():
                    eng.wait_ge(s, v)
                    waited[key] = v
                if op.fn is None:
                    continue
                res = op.fn(eng)
                if op.dma is not None:
                    for ins in res:
                        ins.then_inc(dsem[op.dma], 16)
                elif (e, i) in count:
                    res.then_inc(esem[e], 1)
            if e == "sp":
                for n, v in final.items():
                    if waited.get(("d", n), 0) < v:
                        eng.wait_ge(dsem[n], v)

        block = stack.enter_context(nc.Block())

        @block.tensor
        def _(eng):
            run("pe", eng)

        @block.scalar
        def _(eng):
            run("act", eng)

        @block.vector
        def _(eng):
            run("dve", eng)

        @block.gpsimd
        def _(eng):
            run("pool", eng)

        @block.sync
        def _(eng):
            run("sp", eng)


class Arena:
    def __init__(self, big):
        self.big = big
        self.off = 0
        self.cap = big.shape[1]

    def alloc(self, shape, dtype):
        n = 1
        for s in shape:
            n *= s
        esz = 2 if dtype == BF16 else 4
        n32 = (n * esz + 3) // 4
        n32 = (n32 + 15) // 16 * 16
        assert self.off + n32 <= self.cap, ("SBUF arena overflow", self.off, n32, self.cap)
        ap = self.big[:, self.off:self.off + n32]
        self.off += n32
        if dtype != F32:
            ap = ap.bitcast(dtype)
        ap = ap[:, 0:n]
        if len(shape) == 2:
            ap = ap.rearrange("p (a b) -> p a b", a=shape[0])
        elif len(shape) == 3:
            ap = ap.rearrange("p (a b c) -> p a b c", a=shape[0], b=shape[1])
        return ap

    def mark(self):
        return self.off

    def release(self, m):
        self.off = m


def build(S=SEQ, nph=7, dbg=False, only=None):
    NT = S // T
    nc = bass.Bass("TRN2", target_bir_lowering=False)
    dt_in = lambda name, shape: nc.dram_tensor(name, shape, F32, kind="ExternalInput").ap()
    x_d = dt_in("x", [S, D])
    ctx_d = dt_in("ctx", [CTX, D])
    vecs_d = dt_in("vecs", [NROWS_PAD, P])
    ada_w_d = dt_in("ada_w", [2, D, 9 * D])
    w1_d = dt_in("ffn_w1", [2, 2, D, DFF])
    w3_d = dt_in("ffn_w3", [2, 2, D, DFF])
    w2_d = dt_in("ffn_w2", [2, 2, DFF, D])
    lwin_d = dt_in("lru_w_in", [D, 2 * D])
    lwa_d = dt_in("lru_wa", [2, 8, P, P])
    lwi_d = dt_in("lru_wi", [2, 8, P, P])
    lwo_d = dt_in("lru_w_out", [D, D])
    swin_d = dt_in("sgu_w_in", [D, 4 * D])
    sbv_d = dt_in("sgu_b_in_v", [1, 2 * D])
    sws_d = dt_in("sgu_ws", [8, P, P])
    sbs_d = dt_in("sgu_bs", [8, P])
    swo_d = dt_in("sgu_w_out", [2 * D, D])
    out_d = nc.dram_tensor("out", [S, D], F32, kind="ExternalOutput").ap()
    scr = lambda name, n: nc.dram_tensor(name, [KC, P, n], F32, kind="Internal").ap()
    XA = scr("xa", S)
    XB = scr("xb", S)
    CA = scr("ca", CTX)
    GB = scr("gb", S)
    XC = scr("xc", S)
    HF = scr("hf", S)

    S_ = Sched()
    stack = contextlib.ExitStack()
    big = stack.enter_context(nc.sbuf_tensor("big", [P, 53000], F32))
    ps = stack.enter_context(nc.psum_tensor("ps", [P, 8, 512], F32))
    ar = Arena(big)

    def fm(dram, t0, w):
        return dram[:, :, t0:t0 + w].rearrange("c p t -> p c t")

    ident = ar.alloc([P], F32)
    ones_bf = ar.alloc([P], BF16)
    V = ar.alloc([NROWS_PAD], F32)
    MOD = ar.alloc([2, 72, 2], F32)
    AV = ar.alloc([12, KC], F32)
    SHV = ar.alloc([12, KC], F32)
    GV = ar.alloc([12, KC], F32)
    rowtab = ar.alloc([4, 128], F32)
    coltab = ar.alloc([4, 64], F32)
    cf_t = ar.alloc([2, KC], F32)
    cf2_t = ar.alloc([2, KC], F32)
    st_f = ar.alloc([KC], F32)
    st_b = ar.alloc([KC], F32)
    epsc = ar.alloc([1], F32)
    mhalf = ar.alloc([T + 3], F32)
    base_mark = ar.mark()

    def vcol(name, i=0):
        c = _ROWS[name] + i
        return V[:, c:c + 1]

    def mi(l, k, j):
        return (l * 3 + k) * 2 + j

    def phase0():
        m0 = ar.mark()
        stg = ar.alloc([4, P], F32)
        iot = ar.alloc([P], F32)
        scm = ar.alloc([KC, 2], F32)
        wblk = [ar.alloc([KC, 1024], F32) for _ in range(2)]
        tmpa = ar.alloc([4, 128], F32)
        tmpb = ar.alloc([4, 128], F32)
        tmpi = ar.alloc([4, 128], I32)
        jidx = ar.alloc([2], F32)
        omg = ar.alloc([2], F32)
        pos = ar.alloc([128], F32)
        sp_t = ar.alloc([2, KC], F32)

        S_.add("pool", lambda e: e.iota(iot, [[1, P]], base=0, channel_multiplier=-1,
                                        allow_small_or_imprecise_dtypes=True), w=["iot"])
        S_.add("dve", lambda e: e.tensor_single_scalar(out=ident, in_=iot, scalar=0.0, op=ALU.is_equal),
               r=["iot"], w=["ident"])
        S_.add("dve", lambda e: e.memset(ones_bf, 1.0), w=["ones"])
        S_.add("dve", lambda e: e.memset(epsc, EPS), w=["epsc"])
        S_.add("dve", lambda e: e.memset(mhalf, -0.5), w=["mhalf"])
        S_.add("dve", lambda e: e.memset(st_f, 0.0), w=["st_f"])
        S_.add("dve", lambda e: e.memset(st_b, 0.0), w=["st_b"])
        for q in range(4):
            S_.add("sp", lambda e, q=q: [e.dma_start(out=stg[:, q, :], in_=vecs_d[q * P:(q + 1) * P, :])],
                   w=[("stg", q)], dma="stg%d" % q)
            S_.add("pe", lambda e, q=q: e.transpose(out=ps[:, 7, q * P:(q + 1) * P], in_=stg[:, q, :], identity=ident),
                   r=[("stg", q), "ident"], w=[("ps7", q)])
        S_.add("dve", lambda e: e.tensor_copy(out=V, in_=ps[:, 7, :]), r=[("ps7", q) for q in range(4)], w=["V"])
        S_.add("act", lambda e: e.activation(out=scm[:, :, 0], in_=V[:, _ROWS["c"]:_ROWS["c"] + 8], func=AF.Silu),
               r=["V"], w=["scm0"])
        S_.add("act", lambda e: e.activation(out=scm[:, :, 1], in_=V[:, _ROWS["cctx"]:_ROWS["cctx"] + 8], func=AF.Silu),
               r=["V"], w=["scm1"])
        bi_ = 0
        for l in range(2):
            for nb in range(9):
                slot = bi_ % 2
                bi_ += 1

                def ld(e, l=l, nb=nb, slot=slot):
                    return [e.dma_start(out=wblk[slot][:, k, :],
                                        in_=ada_w_d[l, k * P:(k + 1) * P, nb * 1024:(nb + 1) * 1024]) for k in range(KC)]
                S_.add("sp", ld, w=[("wblk", slot)], dma="wblk%d" % slot, ndma=KC)
                for n in range(8):
                    col = nb * 8 + n

                    def mm(e, l=l, col=col, n=n, slot=slot):
                        for k in range(KC):
                            ins = e.matmul(ps[:, 6, (l * 72 + col) * 2:(l * 72 + col) * 2 + 2],
                                           lhsT=wblk[slot][:, k, n * P:(n + 1) * P], rhs=scm[:, k, :],
                                           start=(k == 0), stop=(k == KC - 1))
                        return ins
                    S_.add("pe", mm, r=[("wblk", slot), "scm0", "scm1"], w=["ps6"])
        modv = MOD.rearrange("p l c j -> p (l c) j")
        psv = ps[:, 6, 0:288].rearrange("p (c j) -> p c j", j=2)
        for j in range(2):
            S_.add("dve", lambda e, j=j: e.tensor_tensor(out=modv[:, :, j], in0=psv[:, :, j],
                                                         in1=V[:, _ROWS["ada_b"]:_ROWS["ada_b"] + 144], op=ALU.add),
                   r=["ps6", "V"], w=[("MOD", j)])
        for l in range(2):
            for k in range(3):
                for j in range(2):
                    m = mi(l, k, j)
                    npre = V[:, _ROWS["npre"] + (l * 3 + k) * 8:_ROWS["npre"] + (l * 3 + k) * 8 + 8]
                    npost = V[:, _ROWS["npost"] + (l * 3 + k) * 8:_ROWS["npost"] + (l * 3 + k) * 8 + 8]
                    sh = MOD[:, l, (3 * k) * 8:(3 * k) * 8 + 8, j]
                    sc = MOD[:, l, (3 * k + 1) * 8:(3 * k + 1) * 8 + 8, j]
                    gt = MOD[:, l, (3 * k + 2) * 8:(3 * k + 2) * 8 + 8, j]
                    wgt = 1.0 if k == 1 else 0.5
                    S_.add("dve", lambda e, m=m, sc=sc, npre=npre: e.scalar_tensor_tensor(
                        out=AV[:, m, :], in0=sc, scalar=1.0, in1=npre, op0=ALU.add, op1=ALU.mult),
                        r=[("MOD", j), "V"], w=[("AV", m)])
                    S_.add("dve", lambda e, m=m, sh=sh: e.tensor_copy(out=SHV[:, m, :], in_=sh),
                           r=[("MOD", j)], w=[("SHV", m)])
                    S_.add("dve", lambda e, m=m, gt=gt, npost=npost, wgt=wgt: e.scalar_tensor_tensor(
                        out=GV[:, m, :], in0=gt, scalar=wgt, in1=npost, op0=ALU.mult, op1=ALU.mult),
                        r=[("MOD", j), "V"], w=[("GV", m)])
        lamv = V[:, _ROWS["lam"]:_ROWS["lam"] + 16]
        spv = sp_t.rearrange("p a b -> p (a b)")
        sw_ = [ar.alloc([16], F32) for _ in range(4)]
        al, ee, ww, w2 = sw_
        S_.add("dve", lambda e: e.tensor_scalar_mul(out=w2, in0=lamv, scalar1=-1.0), r=["V"], w=["w2"])
        S_.add("dve", lambda e: e.tensor_tensor(out=al, in0=lamv, in1=w2, op=ALU.max), r=["V", "w2"], w=["al"])
        S_.add("act", lambda e: e.activation(out=ee, in_=al, func=AF.Exp, scale=-1.0), r=["al"], w=["ee"])
        S_.add("dve", lambda e: e.tensor_scalar_add(out=ww, in0=ee, scalar1=2.0), r=["ee"], w=["ww"])
        S_.add("dve", lambda e: e.reciprocal(out=ww, in_=ww), r=["ww"], w=["ww"])
        S_.add("dve", lambda e: e.tensor_tensor(out=ww, in0=ww, in1=ee, op=ALU.mult), r=["ww", "ee"], w=["ww"])
        S_.add("dve", lambda e: e.tensor_tensor(out=w2, in0=ww, in1=ww, op=ALU.mult), r=["ww"], w=["w2"])
        S_.add("dve", lambda e: e.tensor_scalar(out=spv, in0=w2, scalar1=1.0 / 13.0, scalar2=1.0 / 11.0, op0=ALU.mult, op1=ALU.add),
               r=["w2"], w=["sp_t"])
        for cst in (1.0 / 9.0, 1.0 / 7.0, 1.0 / 5.0, 1.0 / 3.0, 1.0):
            S_.add("dve", lambda e: e.tensor_tensor(out=spv, in0=spv, in1=w2, op=ALU.mult), r=["sp_t", "w2"], w=["sp_t"])
            S_.add("dve", lambda e, cst=cst: e.tensor_scalar_add(out=spv, in0=spv, scalar1=cst), r=["sp_t"], w=["sp_t"])
        S_.add("dve", lambda e: e.scalar_tensor_tensor(out=spv, in0=spv, scalar=2.0, in1=ww, op0=ALU.mult, op1=ALU.mult),
               r=["sp_t", "ww"], w=["sp_t"])
        S_.add("dve", lambda e: e.tensor_scalar(out=al, in0=lamv, scalar1=-1.0, scalar2=0.0, op0=ALU.mult, op1=ALU.max),
               r=["V", "al"], w=["al"])
        S_.add("dve", lambda e: e.tensor_tensor(out=spv, in0=spv, in1=al, op=ALU.add), r=["sp_t", "al"], w=["sp_t"])
        S_.add("dve", lambda e: e.tensor_scalar_mul(out=cf_t.rearrange("p a b -> p (a b)"), in0=spv, scalar1=-RG_C),
               r=["sp_t"], w=["cf"])
        S_.add("dve", lambda e: e.tensor_scalar_mul(out=cf2_t.rearrange("p a b -> p (a b)"), in0=spv, scalar1=-2.0 * RG_C),
               r=["sp_t"], w=["cf2"])
        S_.add("pool", lambda e: e.iota(jidx, [[P, 2]], base=0, channel_multiplier=1,
                                        allow_small_or_imprecise_dtypes=True), w=["jidx"])
        S_.add("pool", lambda e: e.iota(pos, [[1, 128]], base=0, channel_multiplier=0,
                                        allow_small_or_imprecise_dtypes=True), w=["pos"])
        S_.add("act", lambda e: e.activation(out=omg, in_=jidx, func=AF.Exp, scale=-math.log(10000.0) / 256.0),
               r=["jidx"], w=["omg"])
        inv2pi = 1.0 / (2.0 * math.pi)
        for cc in range(4):
            def ang(e, cc=cc):
                return e.tensor_scalar(out=tmpa[:, cc, :], in0=pos, scalar1=omg[:, cc % 2:cc % 2 + 1],
                                       scalar2=inv2pi, op0=ALU.mult, op1=ALU.mult)
            S_.add("dve", ang, r=["pos", "omg"], w=[("tmpa", cc)])
            if cc >= 2:
                S_.add("dve", lambda e, cc=cc: e.tensor_scalar_add(out=tmpa[:, cc, :], in0=tmpa[:, cc, :], scalar1=0.25),
                       r=[("tmpa", cc)], w=[("tmpa", cc)])
        ta = tmpa.rearrange("p a b -> p (a b)")
        tb = tmpb.rearrange("p a b -> p (a b)")
        ti = tmpi.rearrange("p a b -> p (a b)")
        allk = [("tmpa", cc) for cc in range(4)]
        S_.add("dve", lambda e: e.tensor_copy(out=ti, in_=ta), r=allk, w=["tmpi"])
        S_.add("dve", lambda e: e.tensor_copy(out=tb, in_=ti), r=["tmpi"], w=["tmpb"])
        S_.add("dve", lambda e: e.tensor_tensor(out=ta, in0=ta, in1=tb, op=ALU.subtract), r=allk + ["tmpb"], w=allk)
        S_.add("dve", lambda e: e.tensor_single_scalar(out=tb, in_=ta, scalar=0.5, op=ALU.is_gt), r=allk, w=["tmpb"])
        S_.add("dve", lambda e: e.tensor_tensor(out=ta, in0=ta, in1=tb, op=ALU.subtract), r=allk + ["tmpb"], w=allk)
        S_.add("dve", lambda e: e.tensor_single_scalar(out=tb, in_=ta, scalar=-0.5, op=ALU.is_lt), r=allk, w=["tmpb"])
        S_.add("dve", lambda e: e.tensor_tensor(out=ta, in0=ta, in1=tb, op=ALU.add), r=allk + ["tmpb"], w=allk)
        S_.add("act", lambda e: e.activation(out=rowtab.rearrange("p a b -> p (a b)"), in_=ta, func=AF.Sin,
                                             scale=6.283185), r=allk, w=["rowtab"])
        S_.add("act", lambda e: e.activation(out=coltab, in_=tmpa[:, :, 0:64], func=AF.Sin, scale=6.283185),
               r=allk, w=["coltab"])
        S_.barrier()
        ar.release(m0)

    def ffn_phase(l, jf, k_sub, src, dst, do_ctx=False, first=False, last=False):
        m0 = ar.mark()
        w1b = ar.alloc([KC, DFF], BF16)
        w3b = ar.alloc([KC, DFF], BF16)
        w2b = ar.alloc([FC, D], BF16)
        xs = [ar.alloc([KC, T], F32) for _ in range(2)]
        hb = [ar.alloc([KC, T], BF16) for _ in range(2)]
        gbuf = ar.alloc([FC, T], BF16)
        ybuf = ar.alloc([KC, T], F32)
        sqx = ar.alloc([KC, T], BF16)
        sqy = ar.alloc([KC, T], BF16)
        rsx = ar.alloc([T], F32)
        rsy = ar.alloc([T], F32)
        stt = [ar.alloc([T], F32) for _ in range(2)]
        tmp = [ar.alloc([T], F32) for _ in range(2)]
        if first or last:
            xt = ar.alloc([2, D], F32)

        blks = [(0, 6), (6, 11), (11, 17), (17, 22)]
        for bi_, (f0, f1) in enumerate(blks):
            for nm, wd, wb in (("w1", w1_d, w1b), ("w3", w3_d, w3b)):
                def ld(e, wd=wd, wb=wb, f0=f0, f1=f1):
                    return [e.dma_start(out=wb[:, k, f0 * P:f1 * P], in_=wd[l, jf, k * P:(k + 1) * P, f0 * P:f1 * P])
                            for k in range(KC)]
                S_.add("pool", ld, w=[(nm, bi_)], dma="%s_%d" % (nm, bi_), ndma=KC)
        for hh in range(2):
            def ld2(e, hh=hh):
                return [e.dma_start(out=w2b[:, f, :], in_=w2_d[l, jf, f * P:(f + 1) * P, :])
                        for f in range(hh * 11, hh * 11 + 11)]
            S_.add("pool", ld2, w=[("w2", hh)], dma="w2_%d" % hh, ndma=11)
        fblk = {}
        for bi_, (f0, f1) in enumerate(blks):
            for f in range(f0, f1):
                fblk[f] = bi_

        tiles = ([("ctx", 0)] if do_ctx else []) + [("lat", i) for i in range(NT)]
        ntl = len(tiles)

        def load_dma(n):
            kind, i = tiles[n]
            srcd = ctx_d if kind == "ctx" else x_d
            t0 = 0 if kind == "ctx" else i * T
            S_.add("sp", lambda e: [e.dma_start(out=xt[:, tb, :], in_=srcd[t0 + tb * P:t0 + (tb + 1) * P, :])
                                    for tb in range(2)], w=["xt"], dma="xt", ndma=2)

        def load(n):
            kind, i = tiles[n]
            s = n % 2
            xk = [("xs", s, c) for c in range(KC)]
            if first:
                for c2 in range(4):
                    tbk = (7, 4, 5)[c2 % 3]
                    tkey = "ps7" if tbk == 7 else ("psy", tbk)

                    def tr(e, c2=c2, tbk=tbk):
                        for cc in range(2):
                            c = c2 * 2 + cc
                            for tb in range(2):
                                ins = e.transpose(out=ps[:, tbk, cc * T + tb * P:cc * T + (tb + 1) * P],
                                                  in_=xt[:, tb, c * P:(c + 1) * P], identity=ident)
                        return ins
                    S_.add("pe", tr, r=["xt", "ident"], w=[tkey])
                    for cc in range(2):
                        c = c2 * 2 + cc
                        pv = ps[:, tbk, cc * T:(cc + 1) * T]
                        if kind == "ctx":
                            S_.add("dve", lambda e, c=c, pv=pv: e.tensor_copy(out=xs[s][:, c, :], in_=pv),
                                   r=[tkey], w=[xk[c]])
                        elif c < 4:
                            def addr(e, c=c, pv=pv):
                                return e.tensor_tensor(
                                    out=xs[s][:, c, :].rearrange("p (r q) -> p r q", q=64),
                                    in0=pv.rearrange("p (r q) -> p r q", q=64),
                                    in1=rowtab[:, c, 4 * i:4 * i + 4].unsqueeze(2).to_broadcast([P, 4, 64]), op=ALU.add)
                            S_.add("dve", addr, r=[tkey, "rowtab"], w=[xk[c]])
                        else:
                            def addc(e, c=c, pv=pv):
                                return e.tensor_tensor(
                                    out=xs[s][:, c, :].rearrange("p (r q) -> p r q", q=64),
                                    in0=pv.rearrange("p (r q) -> p r q", q=64),
                                    in1=coltab[:, c - 4, :].unsqueeze(1).to_broadcast([P, 4, 64]), op=ALU.add)
                            S_.add("dve", addc, r=[tkey, "coltab"], w=[xk[c]])
            else:
                if kind == "ctx":
                    srcv, dk = fm(CA, 0, T), ("CA",)
                else:
                    srcv, dk = fm(src, i * T, T), (id(src), i)
                S_.add("sp", lambda e: [e.dma_start(out=xs[s], in_=srcv)], r=[dk], w=xk, dma="xs%d" % s)

        def prenorm(n):
            kind, i = tiles[n]
            s = n % 2
            j = 1 if kind == "ctx" else 0
            m = mi(l, k_sub, j)
            x = xs[s]
            h = hb[s]
            for c in range(KC):
                S_.add("pool", lambda e, c=c: e.tensor_tensor(out=sqx[:, c, :], in0=x[:, c, :], in1=x[:, c, :], op=ALU.mult),
                       r=[("xs", s, c)], w=[("sqx", c)])

            def st(e):
                for c in range(KC):
                    ins = e.matmul(ps[:, 6, 0:T], lhsT=ones_bf, rhs=sqx[:, c, :], start=(c == 0), stop=(c == KC - 1))
                return ins
            S_.add("pe", st, r=[("sqx", c) for c in range(KC)] + ["ones"], w=["ps6"])
            S_.add("act", lambda e: e.activation(out=rsx, in_=ps[:, 6, 0:T], func=AF.Sqrt, bias=epsc[:, 0:1], scale=1.0 / D),
               r=["ps6", "epsc"], w=["rsx"])
            S_.add("dve", lambda e: e.reciprocal(out=rsx, in_=rsx), r=["rsx"], w=["rsx"])
            for c in range(KC):
                tt = tmp[c % 2]
                S_.add("dve", lambda e, c=c, tt=tt: e.scalar_tensor_tensor(
                    out=tt, in0=x[:, c, :], scalar=AV[:, m, c:c + 1], in1=rsx, op0=ALU.mult, op1=ALU.mult),
                    r=[("xs", s, c), "rsx", ("AV", m)], w=[("tmp", c % 2)])
                S_.add("act", lambda e, c=c, tt=tt: e.activation(out=h[:, c, :], in_=tt, func=AF.Identity,
                                                                  bias=SHV[:, m, c:c + 1], scale=1.0),
                       r=[("tmp", c % 2), ("SHV", m)], w=[("h", s, c)])

        def up(n, f0, f1):
            s = n % 2
            h = hb[s]
            hk = [("h", s, c) for c in range(KC)]
            for f in range(f0, f1):
                b = f % 4

                def mm(e, f=f, b=b):
                    for k in range(KC):
                        e.matmul(ps[:, b, 0:T], lhsT=w1b[:, k, f * P:(f + 1) * P], rhs=h[:, k, :],
                                 start=(k == 0), stop=(k == KC - 1))
                    for k in range(KC):
                        ins = e.matmul(ps[:, b, T:2 * T], lhsT=w3b[:, k, f * P:(f + 1) * P], rhs=h[:, k, :],
                                       start=(k == 0), stop=(k == KC - 1))
                    return ins
                S_.add("pe", mm, r=hk + [("w1", fblk[f]), ("w3", fblk[f])], w=[("psu", b)])
                sv = stt[f % 2]
                S_.add("act", lambda e, b=b, sv=sv: e.activation(out=sv, in_=ps[:, b, 0:T], func=AF.Silu),
                       r=[("psu", b)], w=[("stt", f % 2)])
                S_.add("dve", lambda e, b=b, sv=sv, f=f: e.tensor_tensor(out=gbuf[:, f, :], in0=sv, in1=ps[:, b, T:2 * T],
                                                                         op=ALU.mult),
                       r=[("psu", b), ("stt", f % 2)], w=[("g", f)])

        def down(n):
            gk = [("g", f) for f in range(FC)]
            for d in range(KC):
                b = 4 + d % 2

                def mm(e, d=d, b=b):
                    for f in range(FC):
                        ins = e.matmul(ps[:, b, 0:T], lhsT=w2b[:, f, d * P:(d + 1) * P], rhs=gbuf[:, f, :],
                                       start=(f == 0), stop=(f == FC - 1))
                    return ins
                S_.add("pe", mm, r=gk + [("w2", 0), ("w2", 1)], w=[("psy", b)])
                S_.add("act", lambda e, d=d, b=b: e.activation(out=ybuf[:, d, :], in_=ps[:, b, 0:T], func=AF.Identity),
                       r=[("psy", b)], w=[("y", d)])
                S_.add("act", lambda e, d=d, b=b: e.activation(out=sqy[:, d, :], in_=ps[:, b, 0:T], func=AF.Square),
                       r=[("psy", b)], w=[("sqy", d)])

        def postnorm(n, part="ab"):
            kind, i = tiles[n]
            s = n % 2
            j = 1 if kind == "ctx" else 0
            m = mi(l, k_sub, j)
            x = xs[s]
            if "a" in part:
                postnorm_a(n, kind, i, s, m, x)
            if "b" in part:
                postnorm_b(n, kind, i, s, x)

        def postnorm_a(n, kind, i, s, m, x):

            def st(e):
                for c in range(KC):
                    ins = e.matmul(ps[:, 6, T:2 * T], lhsT=ones_bf, rhs=sqy[:, c, :], start=(c == 0), stop=(c == KC - 1))
                return ins
            S_.add("pe", st, r=[("sqy", c) for c in range(KC)] + ["ones"], w=["ps6"])
            S_.add("act", lambda e: e.activation(out=rsy, in_=ps[:, 6, T:2 * T], func=AF.Sqrt, bias=epsc[:, 0:1], scale=1.0 / D),
               r=["ps6", "epsc"], w=["rsy"])
            S_.add("dve", lambda e: e.reciprocal(out=rsy, in_=rsy), r=["rsy"], w=["rsy"])
            for c in range(KC):
                tt = tmp[c % 2]
                S_.add("dve", lambda e, c=c, tt=tt: e.scalar_tensor_tensor(
                    out=tt, in0=ybuf[:, c, :], scalar=GV[:, m, c:c + 1], in1=rsy, op0=ALU.mult, op1=ALU.mult),
                    r=[("y", c), "rsy", ("GV", m)], w=[("tmp", c % 2)])
                S_.add("pool", lambda e, c=c, tt=tt: e.tensor_tensor(out=x[:, c, :], in0=x[:, c, :], in1=tt, op=ALU.add),
                       r=[("tmp", c % 2), ("xs", s, c)], w=[("xs", s, c)])
            return

        def postnorm_b(n, kind, i, s, x):
            xk = [("xs", s, c) for c in range(KC)]
            if last:
                t0 = i * T
                for tb in range(2):
                    for c2 in range(2):
                        tbk = (7, 4, 5)[(tb * 2 + c2) % 3]
                        tkey = "ps7" if tbk == 7 else ("psy", tbk)

                        def tr(e, tb=tb, c2=c2, tbk=tbk):
                            for cc in range(4):
                                c = c2 * 4 + cc
                                ins = e.transpose(out=ps[:, tbk, cc * P:(cc + 1) * P], in_=x[:, c, tb * P:(tb + 1) * P],
                                                  identity=ident)
                            return ins
                        S_.add("pe", tr, r=xk + ["ident"], w=[tkey])
                        S_.add("act", lambda e, tb=tb, c2=c2, tbk=tbk: e.activation(out=xt[:, tb, c2 * 512:(c2 + 1) * 512],
                                                                                  in_=ps[:, tbk, :], func=AF.Identity),
                               r=[tkey], w=[("xt", tb, c2)])
                    S_.add("sp", lambda e, tb=tb: [e.dma_start(out=out_d[t0 + tb * P:t0 + (tb + 1) * P, :], in_=xt[:, tb, :])],
                           r=[("xt", tb, 0), ("xt", tb, 1)], w=[("OUT", i, tb)], dma="xt%d" % tb)
            else:
                if kind == "ctx":
                    dv, dk = fm(CA, 0, T), ("CA",)
                else:
                    dv, dk = fm(dst, i * T, T), (id(dst), i)
                S_.add("sp", lambda e: [e.dma_start(out=dv, in_=x)], r=xk, w=[dk], dma="xs%d" % s)

        if first:
            load_dma(0)
        load(0)
        prenorm(0)
        for n in range(ntl):
            if first and n + 1 < ntl:
                load_dma(n + 1)
            if n == 0 and ntl > 1:
                load(1)
            up(n, 0, 6)
            if n >= 1:
                postnorm(n - 1, "a" if last else "ab")
                if not last and n + 1 < ntl:
                    load(n + 1)
            up(n, 6, 13)
            if n >= 1 and last:
                postnorm(n - 1, "b")
                if n + 1 < ntl:
                    load(n + 1)
            up(n, 13, FC)
            if n + 1 < ntl:
                prenorm(n + 1)
            down(n)
        postnorm(ntl - 1)
        S_.barrier()
        ar.release(m0)

    def emit_prenorm(x, xkeys, W, m, h, hkeys, sq, rs, tmp, kt=""):
        for c in range(KC):
            S_.add("pool", lambda e, c=c: e.tensor_tensor(out=sq[:, c, 0:W], in0=x[:, c, 0:W], in1=x[:, c, 0:W], op=ALU.mult),
                   r=[xkeys[c]], w=[("sq" + kt, c)])

        def st(e):
            for c in range(KC):
                ins = e.matmul(ps[:, 6, 0:W], lhsT=ones_bf, rhs=sq[:, c, 0:W], start=(c == 0), stop=(c == KC - 1))
            return ins
        S_.add("pe", st, r=[("sq" + kt, c) for c in range(KC)] + ["ones"], w=["ps6"])
        S_.add("act", lambda e: e.activation(out=rs[:, 0:W], in_=ps[:, 6, 0:W], func=AF.Sqrt, bias=epsc[:, 0:1], scale=1.0 / D),
               r=["ps6", "epsc"], w=["rs" + kt])
        S_.add("dve", lambda e: e.reciprocal(out=rs[:, 0:W], in_=rs[:, 0:W]), r=["rs" + kt], w=["rs" + kt])
        for c in range(KC):
            tt = tmp[c % 2]
            S_.add("dve", lambda e, c=c, tt=tt: e.scalar_tensor_tensor(
                out=tt[:, 0:W], in0=x[:, c, 0:W], scalar=AV[:, m, c:c + 1], in1=rs[:, 0:W], op0=ALU.mult, op1=ALU.mult),
                r=[xkeys[c], "rs" + kt, ("AV", m)], w=[("tmp" + kt, c % 2)])
            S_.add("act", lambda e, c=c, tt=tt: e.activation(out=h[:, c, 0:W], in_=tt[:, 0:W], func=AF.Identity,
                                                              bias=SHV[:, m, c:c + 1], scale=1.0),
                   r=[("tmp" + kt, c % 2), ("SHV", m)], w=[hkeys[c]])

    def emit_out_post(mm_fn, mm_reads, m, x, xkeys, ybuf, sq, rs, tmp, kt="", ybanks=(4, 5)):
        for d in range(KC):
            b = ybanks[d % 2]
            S_.add("pe", lambda e, d=d, b=b: mm_fn(e, d, ps[:, b, 0:T]), r=mm_reads, w=[("psy", b)])
            S_.add("act", lambda e, d=d, b=b: e.activation(out=ybuf[:, d, :], in_=ps[:, b, 0:T], func=AF.Identity),
                   r=[("psy", b)], w=[("y" + kt, d)])
            S_.add("act", lambda e, d=d, b=b: e.activation(out=sq[:, d, 0:T], in_=ps[:, b, 0:T], func=AF.Square),
                   r=[("psy", b)], w=[("sq" + kt, d)])

        def st(e):
            for c in range(KC):
                ins = e.matmul(ps[:, 6, 0:T], lhsT=ones_bf, rhs=sq[:, c, 0:T], start=(c == 0), stop=(c == KC - 1))
            return ins
        S_.add("pe", st, r=[("sq" + kt, c) for c in range(KC)] + ["ones"], w=["ps6"])
        S_.add("act", lambda e: e.activation(out=rs[:, 0:T], in_=ps[:, 6, 0:T], func=AF.Sqrt, bias=epsc[:, 0:1], scale=1.0 / D),
               r=["ps6", "epsc"], w=["rs" + kt])
        S_.add("dve", lambda e: e.reciprocal(out=rs[:, 0:T], in_=rs[:, 0:T]), r=["rs" + kt], w=["rs" + kt])
        for c in range(KC):
            tt = tmp[c % 2]
            S_.add("dve", lambda e, c=c, tt=tt: e.scalar_tensor_tensor(
                out=tt[:, 0:T], in0=ybuf[:, c, :], scalar=GV[:, m, c:c + 1], in1=rs[:, 0:T], op0=ALU.mult, op1=ALU.mult),
                r=[("y" + kt, c), "rs" + kt, ("GV", m)], w=[("tmp" + kt, c % 2)])
            S_.add("pool", lambda e, c=c, tt=tt: e.tensor_tensor(out=x[:, c, 0:T], in0=x[:, c, 0:T], in1=tt[:, 0:T], op=ALU.add),
                   r=[("tmp" + kt, c % 2), xkeys[c]], w=[xkeys[c]])

    def emit_gates_group(dr, heads, xcb, xc, tsg, gs, wab, wib, sk=0):
        for hi, hd in enumerate(heads):
            def mm(e, hi=hi, hd=hd):
                e.matmul(ps[:, hi, 0:T], lhsT=wab[:, dr, hd, :], rhs=xcb[:, hd, :], start=True, stop=True)
                return e.matmul(ps[:, hi, T:2 * T], lhsT=wib[:, dr, hd, :], rhs=xcb[:, hd, :], start=True, stop=True)
            S_.add("pe", mm, r=[("xcb", sk, hd), "wg"], w=[("psg", hi)])
        for hi, hd in enumerate(heads):
            r_t = tsg[hi][0]
            S_.add("act", lambda e, hi=hi, hd=hd, r_t=r_t: e.activation(out=r_t, in_=ps[:, hi, 0:T], func=AF.Sigmoid,
                                                                        bias=vcol("ba", dr * 8 + hd), scale=1.0),
                   r=[("psg", hi)], w=[("r_t", gs, hi)])
        for hi, hd in enumerate(heads):
            i_t = tsg[hi][1]
            S_.add("act", lambda e, hi=hi, hd=hd, i_t=i_t: e.activation(out=i_t, in_=ps[:, hi, T:2 * T], func=AF.Sigmoid,
                                                                        bias=vcol("bi", dr * 8 + hd), scale=1.0),
                   r=[("psg", hi)], w=[("i_t", gs, hi)])
        for hi, hd in enumerate(heads):
            r_t, a_t, e_t = tsg[hi][0], tsg[hi][2], tsg[hi][3]
            S_.add("act", lambda e, hd=hd, r_t=r_t, a_t=a_t: e.activation(out=a_t, in_=r_t, func=AF.Exp,
                                                                          scale=cf_t[:, dr, hd:hd + 1]),
                   r=[("r_t", gs, hi)], w=[("a_t", gs, hi)])
            S_.add("act", lambda e, hd=hd, r_t=r_t, e_t=e_t: e.activation(out=e_t, in_=r_t, func=AF.Exp,
                                                                          scale=cf2_t[:, dr, hd:hd + 1]),
                   r=[("r_t", gs, hi)], w=[("e_t", gs, hi)])
        for hi, hd in enumerate(heads):
            e_t = tsg[hi][3]
            S_.add("act", lambda e, e_t=e_t: e.activation(out=e_t, in_=e_t, func=AF.Sqrt, bias=1.0, scale=-1.0),
                   r=[("e_t", gs, hi)], w=[("e_t", gs, hi)])
        outs = []
        for hi, hd in enumerate(heads):
            i_t, a_t, e_t, b_t = tsg[hi][1], tsg[hi][2], tsg[hi][3], tsg[hi][4]
            S_.add("dve", lambda e, hd=hd, i_t=i_t, b_t=b_t: e.tensor_tensor(out=b_t, in0=i_t, in1=xc[:, hd, :], op=ALU.mult),
                   r=[("i_t", gs, hi), ("xc", sk, hd)], w=[("b_t", gs, hi)])
            S_.add("dve", lambda e, e_t=e_t, b_t=b_t: e.tensor_tensor(out=b_t, in0=b_t, in1=e_t, op=ALU.mult),
                   r=[("b_t", gs, hi), ("e_t", gs, hi)], w=[("b_t", gs, hi)])
            outs.append((a_t, b_t, ("a_t", gs, hi), ("b_t", gs, hi)))
        return outs

    def load_gate_w(wab, wib):
        for nm, wd, wb in (("wa", lwa_d, wab), ("wi", lwi_d, wib)):
            for dr in range(2):
                S_.add("pool", lambda e, wd=wd, wb=wb, dr=dr: [e.dma_start(out=wb[:, dr, :, :],
                                                                           in_=wd[dr].rearrange("h i j -> i h j"))],
                       w=["wg"], dma="%s%d" % (nm, dr))

    def interleave(*gens):
        gens = [g for g in gens if g is not None]
        while gens:
            for g in list(gens):
                try:
                    next(g)
                except StopIteration:
                    gens.remove(g)

    def lru_in_phase():
        m0 = ar.mark()
        W = T + 3
        winb = ar.alloc([KC, 2 * D], BF16)
        wab = ar.alloc([2, 8, P], BF16)
        wib = ar.alloc([2, 8, P], BF16)
        xw = [ar.alloc([KC, W], F32) for _ in range(2)]
        sq = ar.alloc([KC, W], BF16)
        h = ar.alloc([KC, W], BF16)
        zc = ar.alloc([KC, W], F32)
        gbo = [ar.alloc([KC, T], F32) for _ in range(2)]
        xcs = [ar.alloc([KC, T], F32) for _ in range(2)]
        hfo = [ar.alloc([KC, T], F32) for _ in range(2)]
        xcbs = [ar.alloc([KC, T], BF16) for _ in range(2)]
        rs = ar.alloc([W], F32)
        tmp = [ar.alloc([W], F32) for _ in range(2)]
        tsg = [[[ar.alloc([T], F32) for _ in range(5)] for _ in range(4)] for _ in range(2)]
        hbt = ar.alloc([T], F32)
        xbanks = [4, 5, 7]
        xb_i = [0]
        for k in range(KC):
            S_.add("pool", lambda e, k=k: [e.dma_start(out=winb[:, k, hh * D:(hh + 1) * D],
                                                       in_=lwin_d[k * P:(k + 1) * P, hh * D:(hh + 1) * D]) for hh in range(2)],
                   w=[("win", k)], dma="win%d" % k, ndma=2)
        load_gate_w(wab, wib)
        for s in range(2):
            S_.add("dve", lambda e, s=s: e.memset(xw[s], 0.0), w=[("xw", s, c) for c in range(KC)])
        tiles = [("ctx", 0)] + [("lat", i) for i in range(NT)]
        wk = [("win", k) for k in range(KC)]
        hk = [("h", c) for c in range(KC)]

        def stageA(n):
            kind, i = tiles[n]
            s = n % 2
            isctx = kind == "ctx"
            t0 = i * T
            Ssrc = CTX if isctx else S
            lo, hi = max(t0 - 2, 0), min(t0 + T + 1, Ssrc)
            co = lo - (t0 - 2)
            xk = [("xw", s, c) for c in range(KC)]
            srcd = CA if isctx else XA
            S_.add("sp", lambda e: [e.dma_start(out=xw[s][:, :, co:co + hi - lo], in_=fm(srcd, lo, hi - lo))],
                   w=xk, dma="xw%d" % s)
            m = mi(0, 1, 1 if isctx else 0)
            emit_prenorm(xw[s], xk, W, m, h, hk, sq, rs, tmp)
            yield
            if not isctx:
                for fc in range(KC):
                    bk = xbanks[xb_i[0] % 3]
                    xb_i[0] += 1

                    def mm(e, fc=fc, bk=bk):
                        for k in range(KC):
                            ins = e.matmul(ps[:, bk, 0:T], lhsT=winb[:, k, fc * P:(fc + 1) * P], rhs=h[:, k, 2:2 + T],
                                           start=(k == 0), stop=(k == KC - 1))
                        return ins
                    S_.add("pe", mm, r=hk + wk, w=[("psX", bk)])
                    S_.add("act", lambda e, fc=fc, bk=bk: e.activation(out=gbo[s][:, fc, :], in_=ps[:, bk, 0:T],
                                                                       func=AF.Gelu_apprx_tanh),
                           r=[("psX", bk)], w=[("gbo", s, fc)])
                    if fc % 4 == 3:
                        yield
                S_.add("sp", lambda e: [e.dma_start(out=fm(GB, t0, T), in_=gbo[s])],
                       r=[("gbo", s, fc) for fc in range(KC)], w=[("GB", i)], dma="gbo%d" % s)
            for fc in range(KC):
                bk = xbanks[xb_i[0] % 3]
                xb_i[0] += 1

                def mm2(e, fc=fc, bk=bk):
                    for k in range(KC):
                        ins = e.matmul(ps[:, bk, 0:W], lhsT=winb[:, k, D + fc * P:D + (fc + 1) * P], rhs=h[:, k, 0:W],
                                       start=(k == 0), stop=(k == KC - 1))
                    return ins
                S_.add("pe", mm2, r=hk + wk, w=[("psX", bk)])
                S_.add("act", lambda e, fc=fc, bk=bk: e.activation(out=zc[:, fc, :], in_=ps[:, bk, 0:W], func=AF.Identity),
                       r=[("psX", bk)], w=[("zc", fc)])
                if fc % 4 == 3:
                    yield
            zk = [("zc", fc) for fc in range(KC)]
            if co > 0:
                S_.add("dve", lambda e: e.memset(zc[:, :, 0:co], 0.0), r=zk, w=zk)
            if hi - lo + co < W:
                S_.add("dve", lambda e: e.memset(zc[:, :, hi - lo + co:W], 0.0), r=zk, w=zk)
            xc = xcs[s]
            for fc in range(KC):
                S_.add("dve", lambda e, fc=fc: e.tensor_scalar(out=xc[:, fc, :], in0=zc[:, fc, 0:T],
                                                               scalar1=vcol("convw", fc), scalar2=vcol("convb", fc),
                                                               op0=ALU.mult, op1=ALU.add),
                       r=[("zc", fc)], w=[("xc", s, fc)])
                for j in range(1, 4):
                    S_.add("dve", lambda e, fc=fc, j=j: e.scalar_tensor_tensor(
                        out=xc[:, fc, :], in0=zc[:, fc, j:j + T], scalar=vcol("convw", j * 8 + fc), in1=xc[:, fc, :],
                        op0=ALU.mult, op1=ALU.add), r=[("zc", fc), ("xc", s, fc)], w=[("xc", s, fc)])
                S_.add("pool", lambda e, fc=fc: e.tensor_copy(out=xcbs[s][:, fc, :], in_=xc[:, fc, :]),
                       r=[("xc", s, fc)], w=[("xcb", s, fc)])
                if fc % 2 == 1:
                    yield
            if not isctx:
                S_.add("sp", lambda e: [e.dma_start(out=fm(XC, t0, T), in_=xc)],
                       r=[("xc", s, fc) for fc in range(KC)], w=[("XC", i)], dma="xcs%d" % s)

        def stageB(n):
            kind, i = tiles[n]
            s = n % 2
            isctx = kind == "ctx"
            t0 = i * T
            xc = xcs[s]
            xcb = xcbs[s]
            for gs in range(2):
                heads = list(range(gs * 4, gs * 4 + 4))
                res = emit_gates_group(0, heads, xcb, xc, tsg[gs], gs, wab, wib, sk=s)
                yield
                for (a_t, b_t, ak, bk_), hd in zip(res, heads):
                    if n == 0:
                        init, ik = 0.0, []
                    else:
                        init, ik = hfo[1 - s][:, hd, T - 1:T], [("hfo", 1 - s, hd)]
                    S_.add("dve", lambda e, hd=hd, a_t=a_t, b_t=b_t, init=init: e.tensor_tensor_scan(
                        out=hfo[s][:, hd, :], data0=a_t, data1=b_t, initial=init, op0=ALU.mult, op1=ALU.add),
                        r=[ak, bk_] + ik, w=[("hfo", s, hd)])
                yield
            if isctx:
                for gs in range(2):
                    heads = list(range(gs * 4, gs * 4 + 4))
                    res = emit_gates_group(1, heads, xcb, xc, tsg[gs], gs, wab, wib, sk=s)
                    yield
                    for (a_t, b_t, ak, bk_), hd in zip(res, heads):
                        S_.add("dve", lambda e, a_t=a_t, b_t=b_t: e.tensor_tensor_scan(
                            out=hbt[:, ::-1], data0=a_t[:, ::-1], data1=b_t[:, ::-1], initial=0.0, op0=ALU.mult, op1=ALU.add),
                            r=[ak, bk_], w=["hbt"])
                        S_.add("dve", lambda e, hd=hd: e.tensor_copy(out=st_b[:, hd:hd + 1], in_=hbt[:, 0:1]),
                               r=["hbt"], w=[("st_b", hd)])
                    yield
            if not isctx:
                S_.add("sp", lambda e: [e.dma_start(out=fm(HF, t0, T), in_=hfo[s])],
                       r=[("hfo", s, hd) for hd in range(KC)], w=[("HF", i)], dma="hfo%d" % s)

        ntl = len(tiles)
        interleave(stageA(0))
        for n in range(ntl):
            interleave(stageB(n), stageA(n + 1) if n + 1 < ntl else None)
        S_.barrier()
        ar.release(m0)

    def lru_out_phase():
        m0 = ar.mark()
        wob = ar.alloc([KC, D], BF16)
        wab = ar.alloc([2, 8, P], BF16)
        wib = ar.alloc([2, 8, P], BF16)
        xcs = [ar.alloc([KC, T], F32) for _ in range(2)]
        hfs = [ar.alloc([KC, T], F32) for _ in range(2)]
        gbs = [ar.alloc([KC, T], F32) for _ in range(2)]
        x1s = [ar.alloc([KC, T], F32) for _ in range(2)]
        xcbs = [ar.alloc([KC, T], BF16) for _ in range(2)]
        mbs = [ar.alloc([KC, T], BF16) for _ in range(2)]
        ybuf = ar.alloc([KC, T], F32)
        sq = ar.alloc([KC, T], BF16)
        rs = ar.alloc([T], F32)
        tmp = [ar.alloc([T], F32) for _ in range(2)]
        tsg = [[[ar.alloc([T], F32) for _ in range(5)] for _ in range(4)] for _ in range(2)]
        hbs = [ar.alloc([KC, T], F32) for _ in range(2)]
        smt = [ar.alloc([T], F32) for _ in range(2)]
        for k in range(KC):
            S_.add("pool", lambda e, k=k: [e.dma_start(out=wob[:, k, :], in_=lwo_d[k * P:(k + 1) * P, :])],
                   w=[("wo", k)], dma="wo%d" % k)
        load_gate_w(wab, wib)
        m = mi(0, 1, 0)
        order = list(range(NT - 1, -1, -1))

        def stageA(n):
            i = order[n]
            s = n % 2
            t0 = i * T
            for nm, dr_, bufs in (("xc", XC, xcs), ("hf", HF, hfs), ("gb", GB, gbs), ("x1", XA, x1s)):
                S_.add("sp", lambda e, dr_=dr_, bufs=bufs: [e.dma_start(out=bufs[s], in_=fm(dr_, t0, T))],
                       w=[(nm, s, c) for c in range(KC)], dma="%s%d" % (nm, s))
            for c in range(KC):
                S_.add("pool", lambda e, c=c: e.tensor_copy(out=xcbs[s][:, c, :], in_=xcs[s][:, c, :]),
                       r=[("xc", s, c)], w=[("xcb", s, c)])
            yield

        def stageB(n):
            s = n % 2
            for gs in range(2):
                heads = list(range(gs * 4, gs * 4 + 4))
                res = emit_gates_group(1, heads, xcbs[s], xcs[s], tsg[gs], gs, wab, wib, sk=s)
                yield
                for (a_t, b_t, ak, bk_), hd in zip(res, heads):
                    par = hd % 2
                    if n == 0:
                        init, ik = st_b[:, hd:hd + 1], [("st_b", hd)]
                    else:
                        init, ik = hbs[1 - s][:, hd, 0:1], [("hbs", 1 - s, hd)]
                    S_.add("dve", lambda e, hd=hd, a_t=a_t, b_t=b_t, init=init: e.tensor_tensor_scan(
                        out=hbs[s][:, hd, ::-1], data0=a_t[:, ::-1], data1=b_t[:, ::-1], initial=init,
                        op0=ALU.mult, op1=ALU.add), r=[ak, bk_] + ik, w=[("hbs", s, hd)])
                    S_.add("dve", lambda e, hd=hd, par=par: e.tensor_tensor(out=smt[par], in0=hbs[s][:, hd, :],
                                                                          in1=hfs[s][:, hd, :], op=ALU.add),
                           r=[("hbs", s, hd), ("hf", s, hd)], w=[("smt", par)])
                    S_.add("pool", lambda e, hd=hd, par=par: e.tensor_tensor(out=mbs[s][:, hd, :], in0=smt[par],
                                                                           in1=gbs[s][:, hd, :], op=ALU.mult),
                           r=[("smt", par), ("gb", s, hd)], w=[("mb", s, hd)])
                    if hd % 2 == 1:
                        yield

        def stageC(n):
            i = order[n]
            s = n % 2
            t0 = i * T
            mb = mbs[s]

            def mmo(e, d, out):
                for k in range(KC):
                    ins = e.matmul(out, lhsT=wob[:, k, d * P:(d + 1) * P], rhs=mb[:, k, :], start=(k == 0), stop=(k == KC - 1))
                return ins
            xk = [("x1", s, c) for c in range(KC)]
            emit_out_post(mmo, [("mb", s, k) for k in range(KC)] + [("wo", k) for k in range(KC)], m, x1s[s], xk, ybuf, sq, rs, tmp)
            S_.add("sp", lambda e: [e.dma_start(out=fm(XB, t0, T), in_=x1s[s])], r=xk, w=[("XB", i)], dma="x1%d" % s)
            yield

        interleave(stageA(0))
        interleave(stageB(0))
        if NT > 1:
            interleave(stageA(1))
        for n in range(NT):
            interleave(stageC(n), stageB(n + 1) if n + 1 < NT else None)
            if n + 2 < NT:
                interleave(stageA(n + 2))
        S_.barrier()
        ar.release(m0)

    def sgu_phase():
        m0 = ar.mark()
        NV = 2 * D
        swb = ar.alloc([KC, 4 * D], BF16)
        swob = ar.alloc([16, D], BF16)
        wsT = ar.alloc([8, P], BF16)
        bvb = ar.alloc([NV], F32)
        Bm = ar.alloc([16, P], F32)
        xs = [ar.alloc([KC, T], F32) for _ in range(2)]
        h = ar.alloc([KC, T], BF16)
        u = ar.alloc([16, T], BF16)
        vt = ar.alloc([NV], F32)
        vn = ar.alloc([NV], BF16)
        mbufs = [ar.alloc([16, T], BF16) for _ in range(2)]
        ybuf = ar.alloc([KC, T], F32)
        sq = ar.alloc([KC, T], BF16)
        rs = ar.alloc([T], F32)
        tmp = [ar.alloc([T], F32) for _ in range(2)]
        sq2 = ar.alloc([KC, T], BF16)
        rs2 = ar.alloc([T], F32)
        tmp2 = [ar.alloc([T], F32) for _ in range(2)]
        sm4 = [ar.alloc([4, P], F32) for _ in range(2)]
        s1 = ar.alloc([4], F32)
        s2 = ar.alloc([4], F32)
        sc_ = ar.alloc([8], F32)
        ws_f = vt[:, 0:1024].rearrange("p (g q) -> p g q", g=8)
        bsb = vt[:, 1024:2048].rearrange("p (g q) -> p g q", g=8)
        for k in range(KC):
            S_.add("pool", lambda e, k=k: [e.dma_start(out=swb[:, k, hh * D:(hh + 1) * D],
                                                       in_=swin_d[k * P:(k + 1) * P, hh * D:(hh + 1) * D]) for hh in range(4)],
                   w=[("swin", k)], dma="swin%d" % k, ndma=4)
        for hh in range(2):
            S_.add("pool", lambda e, hh=hh: [e.dma_start(out=swob[:, ch, :], in_=swo_d[ch * P:(ch + 1) * P, :])
                                             for ch in range(hh * 8, hh * 8 + 8)], w=[("swo", hh)], dma="swo%d" % hh, ndma=8)
        S_.add("sp", lambda e: [e.dma_start(out=ws_f, in_=sws_d.rearrange("g q p -> q g p"))], w=["ws_f"], dma="ws_f")
        S_.add("sp", lambda e: [e.dma_start(out=bsb, in_=sbs_d.rearrange("g q -> (g q)").partition_broadcast(P))],
               w=["bsb"], dma="bsb")
        S_.add("sp", lambda e: [e.dma_start(out=bvb, in_=sbv_d[0, :].partition_broadcast(P))], w=["bvb"], dma="bvb")
        for g2 in range(2):
            def tr(e, g2=g2):
                for gg in range(4):
                    g = g2 * 4 + gg
                    ins = e.transpose(out=ps[:, 7, gg * P:(gg + 1) * P], in_=ws_f[:, g, :], identity=ident)
                return ins
            S_.add("pe", tr, r=["ws_f", "ident"], w=["ps7"])
            S_.add("dve", lambda e, g2=g2: e.tensor_copy(out=wsT[:, g2 * 4:(g2 + 1) * 4, :].rearrange("p g q -> p (g q)"),
                                                         in_=ps[:, 7, :]), r=["ps7"], w=[("wsT", g2)])
        for g2 in range(2):
            def wsm(e, g2=g2):
                for gg in range(4):
                    ins = e.matmul(ps[:, 6, gg * P:(gg + 1) * P], lhsT=ones_bf, rhs=wsT[:, g2 * 4 + gg, :], start=True, stop=True)
                return ins
            S_.add("pe", wsm, r=[("wsT", g2), "ones"], w=["ps6"])
            for gg in range(4):
                g = g2 * 4 + gg
                for cc in range(2):
                    ch = g * 2 + cc
                    S_.add("dve", lambda e, g=g, gg=gg, ch=ch: e.scalar_tensor_tensor(
                        out=Bm[:, ch, :], in0=ps[:, 6, gg * P:(gg + 1) * P], scalar=vcol("lnb", ch), in1=bsb[:, g, :],
                        op0=ALU.mult, op1=ALU.add), r=["ps6", "bsb"], w=[("Bm", ch)])
        S_.barrier()
        m = mi(1, 1, 0)
        swk = [("swin", k) for k in range(KC)]
        hk = [("h", c) for c in range(KC)]

        def stageA(i):
            s = i % 2
            t0 = i * T
            mbuf = mbufs[s]
            xk = [("xs", s, c) for c in range(KC)]
            S_.add("sp", lambda e: [e.dma_start(out=xs[s], in_=fm(XB, t0, T))], w=xk, dma="xs%d" % s)
            emit_prenorm(xs[s], xk, T, m, h, hk, sq, rs, tmp, kt="A")
            yield

            def vmm(tc):
                for fg in range(4):
                    def mmv(e, fg=fg):
                        for k in range(KC):
                            ins = e.matmul(ps[:, 2 + fg % 2, :], lhsT=h[:, k, tc * P:(tc + 1) * P],
                                           rhs=swb[:, k, NV + fg * 512:NV + (fg + 1) * 512], start=(k == 0), stop=(k == KC - 1))
                        return ins
                    S_.add("pe", mmv, r=hk + swk, w=[("psV", fg % 2)])
                    S_.add("dve", lambda e, fg=fg: e.tensor_tensor(out=vt[:, fg * 512:(fg + 1) * 512], in0=ps[:, 2 + fg % 2, :],
                                                                   in1=bvb[:, fg * 512:(fg + 1) * 512], op=ALU.add),
                           r=[("psV", fg % 2), "bvb"], w=[("vt", fg)])

            def vpost(tc):
                for fg in range(4):
                    S_.add("act", lambda e, fg=fg: e.activation(out=vt[:, fg * 512:(fg + 1) * 512],
                                                                in_=vt[:, fg * 512:(fg + 1) * 512], func=AF.Gelu_apprx_tanh,
                                                                accum_out=s1[:, fg:fg + 1]),
                           r=[("vt", fg)], w=[("vt", fg), ("s1", fg)])
                for fg in range(4):
                    S_.add("act", lambda e, fg=fg: e.activation(out=vn[:, fg * 512:(fg + 1) * 512],
                                                                in_=vt[:, fg * 512:(fg + 1) * 512], func=AF.Square,
                                                                accum_out=s2[:, fg:fg + 1]),
                           r=[("vt", fg)], w=[("vn", fg), ("s2", fg)])
                s1k = [("s1", fg) for fg in range(4)]
                s2k = [("s2", fg) for fg in range(4)]
                S_.add("dve", lambda e: e.reduce_sum(out=sc_[:, 5:6], in_=s1, axis=mybir.AxisListType.X), r=s1k, w=["sc5"])
                S_.add("dve", lambda e: e.reduce_sum(out=sc_[:, 6:7], in_=s2, axis=mybir.AxisListType.X), r=s2k, w=["sc6"])
                S_.add("dve", lambda e: e.tensor_scalar_mul(out=sc_[:, 0:1], in0=sc_[:, 5:6], scalar1=1.0 / NV), r=["sc5"], w=["sc0"])
                S_.add("dve", lambda e: e.tensor_tensor(out=sc_[:, 1:2], in0=sc_[:, 0:1], in1=sc_[:, 0:1], op=ALU.mult),
                       r=["sc0"], w=["sc1"])
                S_.add("dve", lambda e: e.scalar_tensor_tensor(out=sc_[:, 2:3], in0=sc_[:, 6:7], scalar=1.0 / NV, in1=sc_[:, 1:2],
                                                               op0=ALU.mult, op1=ALU.subtract), r=["sc6", "sc1"], w=["sc2"])
                S_.add("act", lambda e: e.activation(out=sc_[:, 2:3], in_=sc_[:, 2:3], func=AF.Sqrt, bias=epsc[:, 0:1], scale=1.0),
                       r=["sc2", "epsc"], w=["sc2"])
                S_.add("dve", lambda e: e.reciprocal(out=sc_[:, 3:4], in_=sc_[:, 2:3]), r=["sc2"], w=["sc3"])
                S_.add("dve", lambda e: e.scalar_tensor_tensor(out=sc_[:, 4:5], in0=sc_[:, 0:1], scalar=-1.0, in1=sc_[:, 3:4],
                                                               op0=ALU.mult, op1=ALU.mult), r=["sc0", "sc3"], w=["sc4"])
                vtk = [("vt", fg) for fg in range(4)]
                vnk = [("vn", fg) for fg in range(4)]
                S_.add("act", lambda e: e.activation(out=vn, in_=vt, func=AF.Identity, bias=sc_[:, 4:5], scale=sc_[:, 3:4]),
                       r=vtk + ["sc3", "sc4"], w=vnk)

            def smm(tc):
                vnk = [("vn", fg) for fg in range(4)]
                for c4 in range(4):
                    par = c4 % 2
                    bk = (5, 7)[par]

                    def mms(e, c4=c4, bk=bk):
                        for cc in range(4):
                            ch = c4 * 4 + cc
                            ins = e.matmul(ps[:, bk, cc * P:(cc + 1) * P], lhsT=vn[:, ch * P:(ch + 1) * P],
                                           rhs=wsT[:, ch // 2, :], start=True, stop=True)
                        return ins
                    S_.add("pe", mms, r=vnk + [("wsT", 0), ("wsT", 1)], w=[("psS", par)])
                    for cc in range(4):
                        ch = c4 * 4 + cc
                        S_.add("dve", lambda e, ch=ch, cc=cc, par=par, bk=bk: e.scalar_tensor_tensor(
                            out=sm4[par][:, cc, :], in0=ps[:, bk, cc * P:(cc + 1) * P], scalar=vcol("lng", ch),
                            in1=Bm[:, ch, :], op0=ALU.mult, op1=ALU.add),
                            r=[("psS", par), ("Bm", ch)], w=[("sm4", par, cc)])
                    S_.add("pool", lambda e, c4=c4, par=par: e.tensor_tensor(
                        out=mbuf[:, c4 * 4:(c4 + 1) * 4, tc * P:(tc + 1) * P], in0=sm4[par],
                        in1=u[:, c4 * 4:(c4 + 1) * 4, tc * P:(tc + 1) * P], op=ALU.mult),
                        r=[("sm4", par, cc) for cc in range(4)] + [("u", c4 * 4 + cc) for cc in range(4)],
                        w=[("m", s, c4, tc)])

            vmm(0)
            yield
            for fc in range(16):
                def mm(e, fc=fc):
                    for k in range(KC):
                        ins = e.matmul(ps[:, fc % 2, 0:T], lhsT=swb[:, k, fc * P:(fc + 1) * P], rhs=h[:, k, :],
                                       start=(k == 0), stop=(k == KC - 1))
                    return ins
                S_.add("pe", mm, r=hk + swk, w=[("psA", fc % 2)])
                S_.add("act", lambda e, fc=fc: e.activation(out=u[:, fc, :], in_=ps[:, fc % 2, 0:T], func=AF.Gelu_apprx_tanh,
                                                            bias=vcol("sbin", fc), scale=1.0),
                       r=[("psA", fc % 2)], w=[("u", fc)])
                if fc == 7:
                    vpost(0)
                if fc % 4 == 3:
                    yield
            smm(0)
            yield
            vmm(1)
            yield
            vpost(1)
            yield
            smm(1)

        def stageB(i):
            s = i % 2
            t0 = i * T
            mbuf = mbufs[s]
            xk = [("xs", s, c) for c in range(KC)]

            def mmo(e, d, out):
                for ch in range(16):
                    ins = e.matmul(out, lhsT=swob[:, ch, d * P:(d + 1) * P], rhs=mbuf[:, ch, :], start=(ch == 0), stop=(ch == 15))
                return ins
            emit_out_post(mmo, [("m", s, c4, tc) for c4 in range(4) for tc in range(2)] + [("swo", 0), ("swo", 1)],
                          m, xs[s], xk, ybuf, sq2, rs2, tmp2, kt="B", ybanks=(4, 4))
            S_.add("sp", lambda e: [e.dma_start(out=fm(XA, t0, T), in_=xs[s])], r=xk, w=[("XA", i)], dma="xs%d" % s)
            yield

        interleave(stageA(0))
        for i in range(NT):
            interleave(stageA(i + 1) if i + 1 < NT else None, stageB(i))
        S_.barrier()
        ar.release(m0)

    dbgs = []

    def dump(n, srcd):
        if dbg:
            dd = nc.dram_tensor("dbg%d" % n, [KC, P, S], F32, kind="ExternalOutput").ap()
            S_.add("sp", lambda e: [e.dma_start(out=dd[:, :, :], in_=srcd[:, :, :])], w=[("DBG", n)], dma="dbg%d" % n)
            S_.barrier()

    phase0()
    if only == 'sgu':
        sgu_phase()
        dump(5, XA)
        nph = 0
    if nph >= 1:
        ffn_phase(0, 0, 0, None, XA, do_ctx=True, first=True)
        dump(1, XA)
    if nph >= 2:
        lru_in_phase()
    if nph >= 3:
        lru_out_phase()
        dump(2, XB)
    if nph >= 4:
        ffn_phase(0, 1, 2, XB, XA)
        dump(3, XA)
    if nph >= 5:
        ffn_phase(1, 0, 0, XA, XB)
        dump(4, XB)
    if nph >= 6:
        sgu_phase()
        dump(5, XA)
    if nph >= 7:
        ffn_phase(1, 1, 2, XA, None, last=True)
    S_.emit(nc, stack)
    stack.close()
    return nc


def make_in_maps(inp, S=SEQ):
    f = lambda a: np.ascontiguousarray(np.asarray(a, dtype=np.float32))
    B = inp["x"].shape[0]
    shared_rows = [
        f(inp["ada_b"]).reshape(144, P), f(inp["norm_pre"]).reshape(48, P), f(inp["norm_post"]).reshape(48, P),
        f(inp["lru_conv_w"]).reshape(32, P), f(inp["lru_conv_b"]).reshape(8, P), f(inp["lru_ba"]).reshape(16, P),
        f(inp["lru_bi"]).reshape(16, P), f(inp["lru_lambda"]).reshape(16, P),
        f(inp["sgu_b_in"])[0, :2 * D].reshape(16, P), f(inp["sgu_ln_g"]).reshape(16, P), f(inp["sgu_ln_b"]).reshape(16, P),
    ]
    shared = {
        "ada_w": f(inp["ada_w"]), "ffn_w1": f(inp["ffn_w1"]), "ffn_w3": f(inp["ffn_w3"]), "ffn_w2": f(inp["ffn_w2"]),
        "lru_w_in": f(inp["lru_w_in"])[0], "lru_wa": f(inp["lru_wa"])[0], "lru_wi": f(inp["lru_wi"])[0],
        "lru_w_out": f(inp["lru_w_out"])[0], "sgu_w_in": f(inp["sgu_w_in"])[0],
        "sgu_b_in_v": f(inp["sgu_b_in"])[:, 2 * D:], "sgu_ws": f(inp["sgu_ws"])[0], "sgu_bs": f(inp["sgu_bs"])[0],
        "sgu_w_out": f(inp["sgu_w_out"])[0],
    }
    maps = []
    for b in range(B):
        vecs = np.zeros((NROWS_PAD, P), np.float32)
        vecs[:NROWS] = np.concatenate(shared_rows + [f(inp["c"])[b].reshape(8, P), f(inp["c_ctx"]).reshape(8, P)], axis=0)
        m = dict(shared)
        m["x"] = f(inp["x"][b, :S])
        m["ctx"] = f(inp["ctx"][b])
        m["vecs"] = vecs
        maps.append(m)
    return maps


_NC_CACHE = {}


def kernel(**inputs):
    maps = make_in_maps(inputs)
    if "nc" not in _NC_CACHE:
        _NC_CACHE["nc"] = build()
    res = run_bass_kernel_spmd(_NC_CACHE["nc"], maps, core_ids=list(range(NCORES)))
    return np.stack([np.asarray(r["out"], dtype=np.float32) for r in res.results], axis=0)
```

```python
import contextlib
import math
import numpy as np
import concourse.bass as bass
import concourse.mybir as mybir
from concourse.bass_utils import run_bass_kernel_spmd

F32 = mybir.dt.float32
BF16 = mybir.dt.bfloat16
I32 = mybir.dt.int32
AF = mybir.ActivationFunctionType
ALU = mybir.AluOpType

P = 128
D = 1024
KC = 8
DFF = 2816
FC = 22
T = 256
CTX = 256
SEQ = 8192
NCORES = 8
EPS = 1e-6
RG_C = 8.0
GRID_W = 64

ENGS = ["pe", "act", "dve", "pool", "sp"]
_DBG = {}

_ROWS = {}
_r = 0
for _name, _n in [("ada_b", 144), ("npre", 48), ("npost", 48), ("convw", 32), ("convb", 8),
                  ("ba", 16), ("bi", 16), ("lam", 16), ("sbin", 16), ("lng", 16), ("lnb", 16),
                  ("c", 8), ("cctx", 8)]:
    _ROWS[_name] = _r
    _r += _n
NROWS = _r
NROWS_PAD = 512


class Op:
    __slots__ = ("fn", "deps", "dma", "tok")

    def __init__(self, fn, deps, dma, tok):
        self.fn, self.deps, self.dma, self.tok = fn, deps, dma, tok


class Sched:
    def __init__(self):
        self.ops = {e: [] for e in ENGS}
        self.lastw = {}
        self.readers = {}
        self.dma_cum = {}

    def add(self, eng, fn, r=(), w=(), dma=None, ndma=1):
        idx = len(self.ops[eng])
        strong = set()
        weak = set()
        for k in r:
            t = self.lastw.get(k)
            if t is not None:
                strong.add(t)
        for k in w:
            t = self.lastw.get(k)
            if t is not None:
                strong.add(t)
            weak.update(self.readers.get(k, ()))
        if dma is not None:
            cum = self.dma_cum.get(dma, 0) + 16 * ndma
            self.dma_cum[dma] = cum
            tok = ("dma", dma, cum)
        else:
            tok = ("eng", eng, idx)
        deps = set()
        for d in strong:
            if d[0] == "eng" and d[1] == eng and eng == "pe":
                continue
            deps.add(d)
        for d in weak:
            if d[0] == "eng" and d[1] == eng and eng == "pe":
                continue
            deps.add(d)
        for k in r:
            self.readers.setdefault(k, []).append(tok)
        for k in w:
            self.lastw[k] = tok
            self.readers[k] = []
        self.ops[eng].append(Op(fn, deps, dma, tok))
        return tok

    def barrier(self):
        toks = set()
        for e in ENGS:
            if self.ops[e]:
                for op in reversed(self.ops[e]):
                    if op.tok is not None and op.tok[0] == "eng":
                        toks.add(op.tok)
                        break
        for name, cum in self.dma_cum.items():
            toks.add(("dma", name, cum))
        for e in ENGS:
            deps = set(t for t in toks if not (t[0] == "eng" and t[1] == e))
            self.ops[e].append(Op(None, deps, None, None))
        self.lastw = {}
        self.readers = {}

    def emit(self, nc, stack):
        needed = set()
        for e in ENGS:
            for op in self.ops[e]:
                for d in op.deps:
                    if d[0] == "eng":
                        needed.add((d[1], d[2]))
        count = {}
        for e in ENGS:
            c = 0
            for i, op in enumerate(self.ops[e]):
                if (e, i) in needed:
                    c += 1
                    count[(e, i)] = c
        esem = {e: stack.enter_context(nc.semaphore("s_" + e)) for e in ENGS}
        dsem = {n: stack.enter_context(nc.semaphore("d_" + n)) for n in self.dma_cum}
        final = dict(self.dma_cum)

        def run(e, eng):
            waited = {}
            for i, op in enumerate(self.ops[e]):
                reqs = {}
                for d in op.deps:
                    if d[0] == "eng":
                        s, v = esem[d[1]], count[(d[1], d[2])]
                        key = ("e", d[1])
                    else:
                        s, v = dsem[d[1]], d[2]
                        key = ("d", d[1])
                    if waited.get(key, 0) >= v:
                        continue
                    if key not in reqs or reqs[key][1] < v:
                        reqs[key] = (s, v)
                for key, (s, v) in reqs.items():
                    eng.wait_ge(s, v)
                    waited[key] = v
                if op.fn is None:
                    continue
                res = op.fn(eng)
                if op.dma is not None:
                    for ins in res:
                        ins.then_inc(dsem[op.dma], 16)
                elif (e, i) in count:
                    res.then_inc(esem[e], 1)
            if e == "sp":
                for n, v in final.items():
                    if waited.get(("d", n), 0) < v:
                        eng.wait_ge(dsem[n], v)

        block = stack.enter_context(nc.Block())

        @block.tensor
        def _(eng):
            run("pe", eng)

        @block.scalar
        def _(eng):
            run("act", eng)

        @block.vector
        def _(eng):
            run("dve", eng)

        @block.gpsimd
        def _(eng):
            run("pool", eng)

        @block.sync
        def _(eng):
            run("sp", eng)


class Arena:
    def __init__(self, big):
        self.big = big
        self.off = 0
        self.cap = big.shape[1]

    def alloc(self, shape, dtype):
        n = 1
        for s in shape:
            n *= s
        esz = 2 if dtype == BF16 else 4
        n32 = (n * esz + 3) // 4
        n32 = (n32 + 15) // 16 * 16
        assert self.off + n32 <= self.cap, ("SBUF arena overflow", self.off, n32, self.cap)
        ap = self.big[:, self.off:self.off + n32]
        self.off += n32
        if dtype != F32:
            ap = ap.bitcast(dtype)
        ap = ap[:, 0:n]
        if len(shape) == 2:
            ap = ap.rearrange("p (a b) -> p a b", a=shape[0])
        elif len(shape) == 3:
            ap = ap.rearrange("p (a b c) -> p a b c", a=shape[0], b=shape[1])
        return ap

    def mark(self):
        return self.off

    def release(self, m):
        self.off = m


def build(S=SEQ, nph=7, dbg=False, only=None):
    NT = S // T
    nc = bass.Bass("TRN2", target_bir_lowering=False)
    dt_in = lambda name, shape: nc.dram_tensor(name, shape, F32, kind="ExternalInput").ap()
    x_d = dt_in("x", [S, D])
    ctx_d = dt_in("ctx", [CTX, D])
    vecs_d = dt_in("vecs", [NROWS_PAD, P])
    ada_w_d = dt_in("ada_w", [2, D, 9 * D])
    w1_d = dt_in("ffn_w1", [2, 2, D, DFF])
    w3_d = dt_in("ffn_w3", [2, 2, D, DFF])
    w2_d = dt_in("ffn_w2", [2, 2, DFF, D])
    lwin_d = dt_in("lru_w_in", [D, 2 * D])
    lwa_d = dt_in("lru_wa", [2, 8, P, P])
    lwi_d = dt_in("lru_wi", [2, 8, P, P])
    lwo_d = dt_in("lru_w_out", [D, D])
    swin_d = dt_in("sgu_w_in", [D, 4 * D])
    sbv_d = dt_in("sgu_b_in_v", [1, 2 * D])
    sws_d = dt_in("sgu_ws", [8, P, P])
    sbs_d = dt_in("sgu_bs", [8, P])
    swo_d = dt_in("sgu_w_out", [2 * D, D])
    out_d = nc.dram_tensor("out", [S, D], F32, kind="ExternalOutput").ap()
    scr = lambda name, n: nc.dram_tensor(name, [KC, P, n], F32, kind="Internal").ap()
    XA = scr("xa", S)
    XB = scr("xb", S)
    CA = scr("ca", CTX)
    GB = scr("gb", S)
    XC = scr("xc", S)
    HF = scr("hf", S)

    S_ = Sched()
    stack = contextlib.ExitStack()
    big = stack.enter_context(nc.sbuf_tensor("big", [P, 53000], F32))
    ps = stack.enter_context(nc.psum_tensor("ps", [P, 8, 512], F32))
    ar = Arena(big)

    def fm(dram, t0, w):
        return dram[:, :, t0:t0 + w].rearrange("c p t -> p c t")

    ident = ar.alloc([P], F32)
    ones_bf = ar.alloc([P], BF16)
    V = ar.alloc([NROWS_PAD], F32)
    MOD = ar.alloc([2, 72, 2], F32)
    AV = ar.alloc([12, KC], F32)
    SHV = ar.alloc([12, KC], F32)
    GV = ar.alloc([12, KC], F32)
    rowtab = ar.alloc([4, 128], F32)
    coltab = ar.alloc([4, 64], F32)
    cf_t = ar.alloc([2, KC], F32)
    cf2_t = ar.alloc([2, KC], F32)
    st_f = ar.alloc([KC], F32)
    st_b = ar.alloc([KC], F32)
    epsc = ar.alloc([1], F32)
    mhalf = ar.alloc([T + 3], F32)
    base_mark = ar.mark()

    def vcol(name, i=0):
        c = _ROWS[name] + i
        return V[:, c:c + 1]

    def mi(l, k, j):
        return (l * 3 + k) * 2 + j

    def phase0():
        m0 = ar.mark()
        stg = ar.alloc([4, P], F32)
        iot = ar.alloc([P], F32)
        scm = ar.alloc([KC, 2], F32)
        wblk = [ar.alloc([KC, 1024], F32) for _ in range(2)]
        tmpa = ar.alloc([4, 128], F32)
        tmpb = ar.alloc([4, 128], F32)
        tmpi = ar.alloc([4, 128], I32)
        jidx = ar.alloc([2], F32)
        omg = ar.alloc([2], F32)
        pos = ar.alloc([128], F32)
        sp_t = ar.alloc([2, KC], F32)

        S_.add("pool", lambda e: e.iota(iot, [[1, P]], base=0, channel_multiplier=-1,
                                        allow_small_or_imprecise_dtypes=True), w=["iot"])
        S_.add("dve", lambda e: e.tensor_single_scalar(out=ident, in_=iot, scalar=0.0, op=ALU.is_equal),
               r=["iot"], w=["ident"])
        S_.add("dve", lambda e: e.memset(ones_bf, 1.0), w=["ones"])
        S_.add("dve", lambda e: e.memset(epsc, EPS), w=["epsc"])
        S_.add("dve", lambda e: e.memset(mhalf, -0.5), w=["mhalf"])
        S_.add("dve", lambda e: e.memset(st_f, 0.0), w=["st_f"])
        S_.add("dve", lambda e: e.memset(st_b, 0.0), w=["st_b"])
        for q in range(4):
            S_.add("sp", lambda e, q=q: [e.dma_start(out=stg[:, q, :], in_=vecs_d[q * P:(q + 1) * P, :])],
                   w=[("stg", q)], dma="stg%d" % q)
            S_.add("pe", lambda e, q=q: e.transpose(out=ps[:, 7, q * P:(q + 1) * P], in_=stg[:, q, :], identity=ident),
                   r=[("stg", q), "ident"], w=[("ps7", q)])
        S_.add("dve", lambda e: e.tensor_copy(out=V, in_=ps[:, 7, :]), r=[("ps7", q) for q in range(4)], w=["V"])
        S_.add("act", lambda e: e.activation(out=scm[:, :, 0], in_=V[:, _ROWS["c"]:_ROWS["c"] + 8], func=AF.Silu),
               r=["V"], w=["scm0"])
        S_.add("act", lambda e: e.activation(out=scm[:, :, 1], in_=V[:, _ROWS["cctx"]:_ROWS["cctx"] + 8], func=AF.Silu),
               r=["V"], w=["scm1"])
        bi_ = 0
        for l in range(2):
            for nb in range(9):
                slot = bi_ % 2
                bi_ += 1

                def ld(e, l=l, nb=nb, slot=slot):
                    return [e.dma_start(out=wblk[slot][:, k, :],
                                        in_=ada_w_d[l, k * P:(k + 1) * P, nb * 1024:(nb + 1) * 1024]) for k in range(KC)]
                S_.add("sp", ld, w=[("wblk", slot)], dma="wblk%d" % slot, ndma=KC)
                for n in range(8):
                    col = nb * 8 + n

                    def mm(e, l=l, col=col, n=n, slot=slot):
                        for k in range(KC):
                            ins = e.matmul(ps[:, 6, (l * 72 + col) * 2:(l * 72 + col) * 2 + 2],
                                           lhsT=wblk[slot][:, k, n * P:(n + 1) * P], rhs=scm[:, k, :],
                                           start=(k == 0), stop=(k == KC - 1))
                        return ins
                    S_.add("pe", mm, r=[("wblk", slot), "scm0", "scm1"], w=["ps6"])
        modv = MOD.rearrange("p l c j -> p (l c) j")
        psv = ps[:, 6, 0:288].rearrange("p (c j) -> p c j", j=2)
        for j in range(2):
            S_.add("dve", lambda e, j=j: e.tensor_tensor(out=modv[:, :, j], in0=psv[:, :, j],
                                                         in1=V[:, _ROWS["ada_b"]:_ROWS["ada_b"] + 144], op=ALU.add),
                   r=["ps6", "V"], w=[("MOD", j)])
        for l in range(2):
            for k in range(3):
                for j in range(2):
                    m = mi(l, k, j)
                    npre = V[:, _ROWS["npre"] + (l * 3 + k) * 8:_ROWS["npre"] + (l * 3 + k) * 8 + 8]
                    npost = V[:, _ROWS["npost"] + (l * 3 + k) * 8:_ROWS["npost"] + (l * 3 + k) * 8 + 8]
                    sh = MOD[:, l, (3 * k) * 8:(3 * k) * 8 + 8, j]
                    sc = MOD[:, l, (3 * k + 1) * 8:(3 * k + 1) * 8 + 8, j]
                    gt = MOD[:, l, (3 * k + 2) * 8:(3 * k + 2) * 8 + 8, j]
                    wgt = 1.0 if k == 1 else 0.5
                    S_.add("dve", lambda e, m=m, sc=sc, npre=npre: e.scalar_tensor_tensor(
                        out=AV[:, m, :], in0=sc, scalar=1.0, in1=npre, op0=ALU.add, op1=ALU.mult),
                        r=[("MOD", j), "V"], w=[("AV", m)])
                    S_.add("dve", lambda e, m=m, sh=sh: e.tensor_copy(out=SHV[:, m, :], in_=sh),
                           r=[("MOD", j)], w=[("SHV", m)])
                    S_.add("dve", lambda e, m=m, gt=gt, npost=npost, wgt=wgt: e.scalar_tensor_tensor(
                        out=GV[:, m, :], in0=gt, scalar=wgt, in1=npost, op0=ALU.mult, op1=ALU.mult),
                        r=[("MOD", j), "V"], w=[("GV", m)])
        lamv = V[:, _ROWS["lam"]:_ROWS["lam"] + 16]
        spv = sp_t.rearrange("p a b -> p (a b)")
        sw_ = [ar.alloc([16], F32) for _ in range(4)]
        al, ee, ww, w2 = sw_
        S_.add("dve", lambda e: e.tensor_scalar_mul(out=w2, in0=lamv, scalar1=-1.0), r=["V"], w=["w2"])
        S_.add("dve", lambda e: e.tensor_tensor(out=al, in0=lamv, in1=w2, op=ALU.max), r=["V", "w2"], w=["al"])
        S_.add("act", lambda e: e.activation(out=ee, in_=al, func=AF.Exp, scale=-1.0), r=["al"], w=["ee"])
        S_.add("dve", lambda e: e.tensor_scalar_add(out=ww, in0=ee, scalar1=2.0), r=["ee"], w=["ww"])
        S_.add("dve", lambda e: e.reciprocal(out=ww, in_=ww), r=["ww"], w=["ww"])
        S_.add("dve", lambda e: e.tensor_tensor(out=ww, in0=ww, in1=ee, op=ALU.mult), r=["ww", "ee"], w=["ww"])
        S_.add("dve", lambda e: e.tensor_tensor(out=w2, in0=ww, in1=ww, op=ALU.mult), r=["ww"], w=["w2"])
        S_.add("dve", lambda e: e.tensor_scalar(out=spv, in0=w2, scalar1=1.0 / 13.0, scalar2=1.0 / 11.0, op0=ALU.mult, op1=ALU.add),
               r=["w2"], w=["sp_t"])
        for cst in (1.0 / 9.0, 1.0 / 7.0, 1.0 / 5.0, 1.0 / 3.0, 1.0):
            S_.add("dve", lambda e: e.tensor_tensor(out=spv, in0=spv, in1=w2, op=ALU.mult), r=["sp_t", "w2"], w=["sp_t"])
            S_.add("dve", lambda e, cst=cst: e.tensor_scalar_add(out=spv, in0=spv, scalar1=cst), r=["sp_t"], w=["sp_t"])
        S_.add("dve", lambda e: e.scalar_tensor_tensor(out=spv, in0=spv, scalar=2.0, in1=ww, op0=ALU.mult, op1=ALU.mult),
               r=["sp_t", "ww"], w=["sp_t"])
        S_.add("dve", lambda e: e.tensor_scalar(out=al, in0=lamv, scalar1=-1.0, scalar2=0.0, op0=ALU.mult, op1=ALU.max),
               r=["V", "al"], w=["al"])
        S_.add("dve", lambda e: e.tensor_tensor(out=spv, in0=spv, in1=al, op=ALU.add), r=["sp_t", "al"], w=["sp_t"])
        S_.add("dve", lambda e: e.tensor_scalar_mul(out=cf_t.rearrange("p a b -> p (a b)"), in0=spv, scalar1=-RG_C),
               r=["sp_t"], w=["cf"])
        S_.add("dve", lambda e: e.tensor_scalar_mul(out=cf2_t.rearrange("p a b -> p (a b)"), in0=spv, scalar1=-2.0 * RG_C),
               r=["sp_t"], w=["cf2"])
        S_.add("pool", lambda e: e.iota(jidx, [[P, 2]], base=0, channel_multiplier=1,
                                        allow_small_or_imprecise_dtypes=True), w=["jidx"])
        S_.add("pool", lambda e: e.iota(pos, [[1, 128]], base=0, channel_multiplier=0,
                                        allow_small_or_imprecise_dtypes=True), w=["pos"])
        S_.add("act", lambda e: e.activation(out=omg, in_=jidx, func=AF.Exp, scale=-math.log(10000.0) / 256.0),
               r=["jidx"], w=["omg"])
        inv2pi = 1.0 / (2.0 * math.pi)
        for cc in range(4):
            def ang(e, cc=cc):
                return e.tensor_scalar(out=tmpa[:, cc, :], in0=pos, scalar1=omg[:, cc % 2:cc % 2 + 1],
                                       scalar2=inv2pi, op0=ALU.mult, op1=ALU.mult)
            S_.add("dve", ang, r=["pos", "omg"], w=[("tmpa", cc)])
            if cc >= 2:
                S_.add("dve", lambda e, cc=cc: e.tensor_scalar_add(out=tmpa[:, cc, :], in0=tmpa[:, cc, :], scalar1=0.25),
                       r=[("tmpa", cc)], w=[("tmpa", cc)])
        ta = tmpa.rearrange("p a b -> p (a b)")
        tb = tmpb.rearrange("p a b -> p (a b)")
        ti = tmpi.rearrange("p a b -> p (a b)")
        allk = [("tmpa", cc) for cc in range(4)]
        S_.add("dve", lambda e: e.tensor_copy(out=ti, in_=ta), r=allk, w=["tmpi"])
        S_.add("dve", lambda e: e.tensor_copy(out=tb, in_=ti), r=["tmpi"], w=["tmpb"])
        S_.add("dve", lambda e: e.tensor_tensor(out=ta, in0=ta, in1=tb, op=ALU.subtract), r=allk + ["tmpb"], w=allk)
        S_.add("dve", lambda e: e.tensor_single_scalar(out=tb, in_=ta, scalar=0.5, op=ALU.is_gt), r=allk, w=["tmpb"])
        S_.add("dve", lambda e: e.tensor_tensor(out=ta, in0=ta, in1=tb, op=ALU.subtract), r=allk + ["tmpb"], w=allk)
        S_.add("dve", lambda e: e.tensor_single_scalar(out=tb, in_=ta, scalar=-0.5, op=ALU.is_lt), r=allk, w=["tmpb"])
        S_.add("dve", lambda e: e.tensor_tensor(out=ta, in0=ta, in1=tb, op=ALU.add), r=allk + ["tmpb"], w=allk)
        S_.add("act", lambda e: e.activation(out=rowtab.rearrange("p a b -> p (a b)"), in_=ta, func=AF.Sin,
                                             scale=6.283185), r=allk, w=["rowtab"])
        S_.add("act", lambda e: e.activation(out=coltab, in_=tmpa[:, :, 0:64], func=AF.Sin, scale=6.283185),
               r=allk, w=["coltab"])
        S_.barrier()
        ar.release(m0)

    def ffn_phase(l, jf, k_sub, src, dst, do_ctx=False, first=False, last=False, barrier_after=True):
        m0 = ar.mark()
        w1b = ar.alloc([KC, DFF], BF16)
        w3b = ar.alloc([KC, DFF], BF16)
        w2b = ar.alloc([FC, D], BF16)
        xs = [ar.alloc([KC, T], F32) for _ in range(2)]
        hb = [ar.alloc([KC, T], BF16) for _ in range(2)]
        gbuf = ar.alloc([FC, T], BF16)
        ybuf = ar.alloc([KC, T], F32)
        sqx = ar.alloc([KC, T], BF16)
        sqy = ar.alloc([KC, T], BF16)
        rsx = ar.alloc([T], F32)
        rsy = ar.alloc([T], F32)
        stt = [ar.alloc([T], F32) for _ in range(2)]
        tmp = [ar.alloc([T], F32) for _ in range(2)]
        if first or last:
            xt = ar.alloc([2, D], F32)

        blks = [(0, 6), (6, 11), (11, 17), (17, 22)]
        for bi_, (f0, f1) in enumerate(blks):
            for nm, wd, wb in (("w1", w1_d, w1b), ("w3", w3_d, w3b)):
                def ld(e, wd=wd, wb=wb, f0=f0, f1=f1):
                    return [e.dma_start(out=wb[:, k, f0 * P:f1 * P], in_=wd[l, jf, k * P:(k + 1) * P, f0 * P:f1 * P])
                            for k in range(KC)]
                S_.add("pool", ld, w=[(nm, bi_)], dma="%s_%d" % (nm, bi_), ndma=KC)
        for hh in range(2):
            def ld2(e, hh=hh):
                return [e.dma_start(out=w2b[:, f, :], in_=w2_d[l, jf, f * P:(f + 1) * P, :])
                        for f in range(hh * 11, hh * 11 + 11)]
            S_.add("pool", ld2, w=[("w2", hh)], dma="w2_%d" % hh, ndma=11)
        fblk = {}
        for bi_, (f0, f1) in enumerate(blks):
            for f in range(f0, f1):
                fblk[f] = bi_

        tiles = ([("ctx", 0)] if do_ctx else []) + [("lat", i) for i in range(NT)]
        ntl = len(tiles)

        def load_dma(n):
            kind, i = tiles[n]
            srcd = ctx_d if kind == "ctx" else x_d
            t0 = 0 if kind == "ctx" else i * T
            S_.add("sp", lambda e: [e.dma_start(out=xt[:, tb, :], in_=srcd[t0 + tb * P:t0 + (tb + 1) * P, :])
                                    for tb in range(2)], w=["xt"], dma="xt", ndma=2)

        def load(n):
            kind, i = tiles[n]
            s = n % 2
            xk = [("xs", s, c) for c in range(KC)]
            if first:
                for c2 in range(4):
                    tbk = (7, 4, 5)[c2 % 3]
                    tkey = "ps7" if tbk == 7 else ("psy", tbk)

                    def tr(e, c2=c2, tbk=tbk):
                        for cc in range(2):
                            c = c2 * 2 + cc
                            for tb in range(2):
                                ins = e.transpose(out=ps[:, tbk, cc * T + tb * P:cc * T + (tb + 1) * P],
                                                  in_=xt[:, tb, c * P:(c + 1) * P], identity=ident)
                        return ins
                    S_.add("pe", tr, r=["xt", "ident"], w=[tkey])
                    for cc in range(2):
                        c = c2 * 2 + cc
                        pv = ps[:, tbk, cc * T:(cc + 1) * T]
                        if kind == "ctx":
                            S_.add("dve", lambda e, c=c, pv=pv: e.tensor_copy(out=xs[s][:, c, :], in_=pv),
                                   r=[tkey], w=[xk[c]])
                        elif c < 4:
                            def addr(e, c=c, pv=pv):
                                return e.tensor_tensor(
                                    out=xs[s][:, c, :].rearrange("p (r q) -> p r q", q=64),
                                    in0=pv.rearrange("p (r q) -> p r q", q=64),
                                    in1=rowtab[:, c, 4 * i:4 * i + 4].unsqueeze(2).to_broadcast([P, 4, 64]), op=ALU.add)
                            S_.add("dve", addr, r=[tkey, "rowtab"], w=[xk[c]])
                        else:
                            def addc(e, c=c, pv=pv):
                                return e.tensor_tensor(
                                    out=xs[s][:, c, :].rearrange("p (r q) -> p r q", q=64),
                                    in0=pv.rearrange("p (r q) -> p r q", q=64),
                                    in1=coltab[:, c - 4, :].unsqueeze(1).to_broadcast([P, 4, 64]), op=ALU.add)
                            S_.add("dve", addc, r=[tkey, "coltab"], w=[xk[c]])
            else:
                if kind == "ctx":
                    srcv, dk = fm(CA, 0, T), ("CA",)
                else:
                    srcv, dk = fm(src, i * T, T), (id(src), i)
                S_.add("sp", lambda e: [e.dma_start(out=xs[s], in_=srcv)], r=[dk], w=xk, dma="xs%d" % s)

        def prenorm(n):
            kind, i = tiles[n]
            s = n % 2
            j = 1 if kind == "ctx" else 0
            m = mi(l, k_sub, j)
            x = xs[s]
            h = hb[s]
            for c in range(KC):
                S_.add("pool", lambda e, c=c: e.tensor_tensor(out=sqx[:, c, :], in0=x[:, c, :], in1=x[:, c, :], op=ALU.mult),
                       r=[("xs", s, c)], w=[("sqx", c)])

            def st(e):
                for c in range(KC):
                    ins = e.matmul(ps[:, 6, 0:T], lhsT=ones_bf, rhs=sqx[:, c, :], start=(c == 0), stop=(c == KC - 1))
                return ins
            S_.add("pe", st, r=[("sqx", c) for c in range(KC)] + ["ones"], w=["ps6"])
            S_.add("act", lambda e: e.activation(out=rsx, in_=ps[:, 6, 0:T], func=AF.Sqrt, bias=epsc[:, 0:1], scale=1.0 / D),
               r=["ps6", "epsc"], w=["rsx"])
            S_.add("dve", lambda e: e.reciprocal(out=rsx, in_=rsx), r=["rsx"], w=["rsx"])
            for c in range(KC):
                tt = tmp[c % 2]
                S_.add("dve", lambda e, c=c, tt=tt: e.scalar_tensor_tensor(
                    out=tt, in0=x[:, c, :], scalar=AV[:, m, c:c + 1], in1=rsx, op0=ALU.mult, op1=ALU.mult),
                    r=[("xs", s, c), "rsx", ("AV", m)], w=[("tmp", c % 2)])
                S_.add("act", lambda e, c=c, tt=tt: e.activation(out=h[:, c, :], in_=tt, func=AF.Identity,
                                                                  bias=SHV[:, m, c:c + 1], scale=1.0),
                       r=[("tmp", c % 2), ("SHV", m)], w=[("h", s, c)])

        def up(n, f0, f1):
            s = n % 2
            h = hb[s]
            hk = [("h", s, c) for c in range(KC)]
            for f in range(f0, f1):
                b = f % 4

                def mm(e, f=f, b=b):
                    for k in range(KC):
                        e.matmul(ps[:, b, 0:T], lhsT=w1b[:, k, f * P:(f + 1) * P], rhs=h[:, k, :],
                                 start=(k == 0), stop=(k == KC - 1))
                    for k in range(KC):
                        ins = e.matmul(ps[:, b, T:2 * T], lhsT=w3b[:, k, f * P:(f + 1) * P], rhs=h[:, k, :],
                                       start=(k == 0), stop=(k == KC - 1))
                    return ins
                S_.add("pe", mm, r=hk + [("w1", fblk[f]), ("w3", fblk[f])], w=[("psu", b)])
                sv = stt[f % 2]
                S_.add("act", lambda e, b=b, sv=sv: e.activation(out=sv, in_=ps[:, b, 0:T], func=AF.Silu),
                       r=[("psu", b)], w=[("stt", f % 2)])
                S_.add("dve", lambda e, b=b, sv=sv, f=f: e.tensor_tensor(out=gbuf[:, f, :], in0=sv, in1=ps[:, b, T:2 * T],
                                                                         op=ALU.mult),
                       r=[("psu", b), ("stt", f % 2)], w=[("g", f)])

        def down(n):
            gk = [("g", f) for f in range(FC)]
            for d in range(KC):
                b = 4 + d % 2

                def mm(e, d=d, b=b):
                    for f in range(FC):
                        ins = e.matmul(ps[:, b, 0:T], lhsT=w2b[:, f, d * P:(d + 1) * P], rhs=gbuf[:, f, :],
                                       start=(f == 0), stop=(f == FC - 1))
                    return ins
                S_.add("pe", mm, r=gk + [("w2", 0), ("w2", 1)], w=[("psy", b)])
                S_.add("act", lambda e, d=d, b=b: e.activation(out=ybuf[:, d, :], in_=ps[:, b, 0:T], func=AF.Identity),
                       r=[("psy", b)], w=[("y", d)])
                S_.add("act", lambda e, d=d, b=b: e.activation(out=sqy[:, d, :], in_=ps[:, b, 0:T], func=AF.Square),
                       r=[("psy", b)], w=[("sqy", d)])

        def postnorm(n, part="ab"):
            kind, i = tiles[n]
            s = n % 2
            j = 1 if kind == "ctx" else 0
            m = mi(l, k_sub, j)
            x = xs[s]
            if "a" in part:
                postnorm_a(n, kind, i, s, m, x)
            if "b" in part:
                postnorm_b(n, kind, i, s, x)

        def postnorm_a(n, kind, i, s, m, x):

            def st(e):
                for c in range(KC):
                    ins = e.matmul(ps[:, 6, T:2 * T], lhsT=ones_bf, rhs=sqy[:, c, :], start=(c == 0), stop=(c == KC - 1))
                return ins
            S_.add("pe", st, r=[("sqy", c) for c in range(KC)] + ["ones"], w=["ps6"])
            S_.add("act", lambda e: e.activation(out=rsy, in_=ps[:, 6, T:2 * T], func=AF.Sqrt, bias=epsc[:, 0:1], scale=1.0 / D),
               r=["ps6", "epsc"], w=["rsy"])
            S_.add("dve", lambda e: e.reciprocal(out=rsy, in_=rsy), r=["rsy"], w=["rsy"])
            for c in range(KC):
                tt = tmp[c % 2]
                S_.add("dve", lambda e, c=c, tt=tt: e.scalar_tensor_tensor(
                    out=tt, in0=ybuf[:, c, :], scalar=GV[:, m, c:c + 1], in1=rsy, op0=ALU.mult, op1=ALU.mult),
                    r=[("y", c), "rsy", ("GV", m)], w=[("tmp", c % 2)])
                S_.add("pool", lambda e, c=c, tt=tt: e.tensor_tensor(out=x[:, c, :], in0=x[:, c, :], in1=tt, op=ALU.add),
                       r=[("tmp", c % 2), ("xs", s, c)], w=[("xs", s, c)])
            return

        def postnorm_b(n, kind, i, s, x):
            xk = [("xs", s, c) for c in range(KC)]
            if last:
                t0 = i * T
                for tb in range(2):
                    for c2 in range(2):
                        tbk = (7, 4, 5)[(tb * 2 + c2) % 3]
                        tkey = "ps7" if tbk == 7 else ("psy", tbk)

                        def tr(e, tb=tb, c2=c2, tbk=tbk):
                            for cc in range(4):
                                c = c2 * 4 + cc
                                ins = e.transpose(out=ps[:, tbk, cc * P:(cc + 1) * P], in_=x[:, c, tb * P:(tb + 1) * P],
                                                  identity=ident)
                            return ins
                        S_.add("pe", tr, r=xk + ["ident"], w=[tkey])
                        S_.add("act", lambda e, tb=tb, c2=c2, tbk=tbk: e.activation(out=xt[:, tb, c2 * 512:(c2 + 1) * 512],
                                                                                  in_=ps[:, tbk, :], func=AF.Identity),
                               r=[tkey], w=[("xt", tb, c2)])
                    S_.add("sp", lambda e, tb=tb: [e.dma_start(out=out_d[t0 + tb * P:t0 + (tb + 1) * P, :], in_=xt[:, tb, :])],
                           r=[("xt", tb, 0), ("xt", tb, 1)], w=[("OUT", i, tb)], dma="xt%d" % tb)
            else:
                if kind == "ctx":
                    dv, dk = fm(CA, 0, T), ("CA",)
                else:
                    dv, dk = fm(dst, i * T, T), (id(dst), i)
                S_.add("sp", lambda e: [e.dma_start(out=dv, in_=x)], r=xk, w=[dk], dma="xs%d" % s)

        if first:
            load_dma(0)
        load(0)
        prenorm(0)
        for n in range(ntl):
            if first and n + 1 < ntl:
                load_dma(n + 1)
            if n == 0 and ntl > 1:
                load(1)
            up(n, 0, 6)
            if n >= 1:
                postnorm(n - 1, "a" if last else "ab")
                if not last and n + 1 < ntl:
                    load(n + 1)
            up(n, 6, 13)
            if n >= 1 and last:
                postnorm(n - 1, "b")
                if n + 1 < ntl:
                    load(n + 1)
            up(n, 13, FC)
            if n + 1 < ntl:
                prenorm(n + 1)
            down(n)
        postnorm(ntl - 1)
        if barrier_after:
            S_.barrier()
        ar.release(m0)

    def emit_prenorm(x, xkeys, W, m, h, hkeys, sq, rs, tmp, kt=""):
        for c in range(KC):
            S_.add("pool", lambda e, c=c: e.tensor_tensor(out=sq[:, c, 0:W], in0=x[:, c, 0:W], in1=x[:, c, 0:W], op=ALU.mult),
                   r=[xkeys[c]], w=[("sq" + kt, c)])

        def st(e):
            for c in range(KC):
                ins = e.matmul(ps[:, 6, 0:W], lhsT=ones_bf, rhs=sq[:, c, 0:W], start=(c == 0), stop=(c == KC - 1))
            return ins
        S_.add("pe", st, r=[("sq" + kt, c) for c in range(KC)] + ["ones"], w=["ps6"])
        S_.add("act", lambda e: e.activation(out=rs[:, 0:W], in_=ps[:, 6, 0:W], func=AF.Sqrt, bias=epsc[:, 0:1], scale=1.0 / D),
               r=["ps6", "epsc"], w=["rs" + kt])
        S_.add("dve", lambda e: e.reciprocal(out=rs[:, 0:W], in_=rs[:, 0:W]), r=["rs" + kt], w=["rs" + kt])
        for c in range(KC):
            tt = tmp[c % 2]
            S_.add("dve", lambda e, c=c, tt=tt: e.scalar_tensor_tensor(
                out=tt[:, 0:W], in0=x[:, c, 0:W], scalar=AV[:, m, c:c + 1], in1=rs[:, 0:W], op0=ALU.mult, op1=ALU.mult),
                r=[xkeys[c], "rs" + kt, ("AV", m)], w=[("tmp" + kt, c % 2)])
            S_.add("act", lambda e, c=c, tt=tt: e.activation(out=h[:, c, 0:W], in_=tt[:, 0:W], func=AF.Identity,
                                                              bias=SHV[:, m, c:c + 1], scale=1.0),
                   r=[("tmp" + kt, c % 2), ("SHV", m)], w=[hkeys[c]])

    def emit_out_post(mm_fn, mm_reads, m, x, xkeys, ybuf, sq, rs, tmp, kt="", ybanks=(4, 5)):
        for d in range(KC):
            b = ybanks[d % 2]
            S_.add("pe", lambda e, d=d, b=b: mm_fn(e, d, ps[:, b, 0:T]), r=mm_reads, w=[("psy", b)])
            S_.add("act", lambda e, d=d, b=b: e.activation(out=ybuf[:, d, :], in_=ps[:, b, 0:T], func=AF.Identity),
                   r=[("psy", b)], w=[("y" + kt, d)])
            S_.add("act", lambda e, d=d, b=b: e.activation(out=sq[:, d, 0:T], in_=ps[:, b, 0:T], func=AF.Square),
                   r=[("psy", b)], w=[("sq" + kt, d)])

        def st(e):
            for c in range(KC):
                ins = e.matmul(ps[:, 6, 0:T], lhsT=ones_bf, rhs=sq[:, c, 0:T], start=(c == 0), stop=(c == KC - 1))
            return ins
        S_.add("pe", st, r=[("sq" + kt, c) for c in range(KC)] + ["ones"], w=["ps6"])
        S_.add("act", lambda e: e.activation(out=rs[:, 0:T], in_=ps[:, 6, 0:T], func=AF.Sqrt, bias=epsc[:, 0:1], scale=1.0 / D),
               r=["ps6", "epsc"], w=["rs" + kt])
        S_.add("dve", lambda e: e.reciprocal(out=rs[:, 0:T], in_=rs[:, 0:T]), r=["rs" + kt], w=["rs" + kt])
        for c in range(KC):
            tt = tmp[c % 2]
            S_.add("dve", lambda e, c=c, tt=tt: e.scalar_tensor_tensor(
                out=tt[:, 0:T], in0=ybuf[:, c, :], scalar=GV[:, m, c:c + 1], in1=rs[:, 0:T], op0=ALU.mult, op1=ALU.mult),
                r=[("y" + kt, c), "rs" + kt, ("GV", m)], w=[("tmp" + kt, c % 2)])
            S_.add("pool", lambda e, c=c, tt=tt: e.tensor_tensor(out=x[:, c, 0:T], in0=x[:, c, 0:T], in1=tt[:, 0:T], op=ALU.add),
                   r=[("tmp" + kt, c % 2), xkeys[c]], w=[xkeys[c]])

    def emit_gates_group(dr, heads, xcb, xc, tsg, gs, wab, wib, sk=0):
        for hi, hd in enumerate(heads):
            def mm(e, hi=hi, hd=hd):
                e.matmul(ps[:, hi, 0:T], lhsT=wab[:, dr, hd, :], rhs=xcb[:, hd, :], start=True, stop=True)
                return e.matmul(ps[:, hi, T:2 * T], lhsT=wib[:, dr, hd, :], rhs=xcb[:, hd, :], start=True, stop=True)
            S_.add("pe", mm, r=[("xcb", sk, hd), "wg"], w=[("psg", hi)])
        for hi, hd in enumerate(heads):
            r_t = tsg[hi][0]
            S_.add("act", lambda e, hi=hi, hd=hd, r_t=r_t: e.activation(out=r_t, in_=ps[:, hi, 0:T], func=AF.Sigmoid,
                                                                        bias=vcol("ba", dr * 8 + hd), scale=1.0),
                   r=[("psg", hi)], w=[("r_t", gs, hi)])
        for hi, hd in enumerate(heads):
            i_t = tsg[hi][1]
            S_.add("act", lambda e, hi=hi, hd=hd, i_t=i_t: e.activation(out=i_t, in_=ps[:, hi, T:2 * T], func=AF.Sigmoid,
                                                                        bias=vcol("bi", dr * 8 + hd), scale=1.0),
                   r=[("psg", hi)], w=[("i_t", gs, hi)])
        for hi, hd in enumerate(heads):
            r_t, a_t, e_t = tsg[hi][0], tsg[hi][2], tsg[hi][3]
            S_.add("act", lambda e, hd=hd, r_t=r_t, a_t=a_t: e.activation(out=a_t, in_=r_t, func=AF.Exp,
                                                                          scale=cf_t[:, dr, hd:hd + 1]),
                   r=[("r_t", gs, hi)], w=[("a_t", gs, hi)])
            S_.add("act", lambda e, hd=hd, r_t=r_t, e_t=e_t: e.activation(out=e_t, in_=r_t, func=AF.Exp,
                                                                          scale=cf2_t[:, dr, hd:hd + 1]),
                   r=[("r_t", gs, hi)], w=[("e_t", gs, hi)])
        for hi, hd in enumerate(heads):
            e_t = tsg[hi][3]
            S_.add("act", lambda e, e_t=e_t: e.activation(out=e_t, in_=e_t, func=AF.Sqrt, bias=1.0, scale=-1.0),
                   r=[("e_t", gs, hi)], w=[("e_t", gs, hi)])
        outs = []
        for hi, hd in enumerate(heads):
            i_t, a_t, e_t, b_t = tsg[hi][1], tsg[hi][2], tsg[hi][3], tsg[hi][4]
            S_.add("dve", lambda e, hd=hd, i_t=i_t, b_t=b_t: e.tensor_tensor(out=b_t, in0=i_t, in1=xc[:, hd, :], op=ALU.mult),
                   r=[("i_t", gs, hi), ("xc", sk, hd)], w=[("b_t", gs, hi)])
            S_.add("dve", lambda e, e_t=e_t, b_t=b_t: e.tensor_tensor(out=b_t, in0=b_t, in1=e_t, op=ALU.mult),
                   r=[("b_t", gs, hi), ("e_t", gs, hi)], w=[("b_t", gs, hi)])
            outs.append((a_t, b_t, ("a_t", gs, hi), ("b_t", gs, hi)))
        return outs

    def load_gate_w(wab, wib):
        for nm, wd, wb in (("wa", lwa_d, wab), ("wi", lwi_d, wib)):
            for dr in range(2):
                S_.add("pool", lambda e, wd=wd, wb=wb, dr=dr: [e.dma_start(out=wb[:, dr, :, :],
                                                                           in_=wd[dr].rearrange("h i j -> i h j"))],
                       w=["wg"], dma="%s%d" % (nm, dr))

    def interleave(*gens):
        gens = [g for g in gens if g is not None]
        while gens:
            for g in list(gens):
                try:
                    next(g)
                except StopIteration:
                    gens.remove(g)

    def lru_in_phase():
        m0 = ar.mark()
        W = T + 3
        winb = ar.alloc([KC, 2 * D], BF16)
        wab = ar.alloc([2, 8, P], BF16)
        wib = ar.alloc([2, 8, P], BF16)
        xw = [ar.alloc([KC, W], F32) for _ in range(2)]
        sq = ar.alloc([KC, W], BF16)
        h = ar.alloc([KC, W], BF16)
        zc = ar.alloc([KC, W], F32)
        gbo = [ar.alloc([KC, T], F32) for _ in range(2)]
        xcs = [ar.alloc([KC, T], F32) for _ in range(2)]
        hfo = [ar.alloc([KC, T], F32) for _ in range(2)]
        xcbs = [ar.alloc([KC, T], BF16) for _ in range(2)]
        rs = ar.alloc([W], F32)
        tmp = [ar.alloc([W], F32) for _ in range(2)]
        tsg = [[[ar.alloc([T], F32) for _ in range(5)] for _ in range(4)] for _ in range(2)]
        hbt = ar.alloc([T], F32)
        xbanks = [4, 5, 7]
        xb_i = [0]
        for k in range(KC):
            S_.add("pool", lambda e, k=k: [e.dma_start(out=winb[:, k, hh * D:(hh + 1) * D],
                                                       in_=lwin_d[k * P:(k + 1) * P, hh * D:(hh + 1) * D]) for hh in range(2)],
                   w=[("win", k)], dma="win%d" % k, ndma=2)
        load_gate_w(wab, wib)
        for s in range(2):
            S_.add("dve", lambda e, s=s: e.memset(xw[s], 0.0), w=[("xw", s, c) for c in range(KC)])
        tiles = [("ctx", 0)] + [("lat", i) for i in range(NT)]
        wk = [("win", k) for k in range(KC)]
        hk = [("h", c) for c in range(KC)]

        def stageA(n):
            kind, i = tiles[n]
            s = n % 2
            isctx = kind == "ctx"
            t0 = i * T
            Ssrc = CTX if isctx else S
            lo, hi = max(t0 - 2, 0), min(t0 + T + 1, Ssrc)
            co = lo - (t0 - 2)
            xk = [("xw", s, c) for c in range(KC)]
            srcd = CA if isctx else XA
            S_.add("sp", lambda e: [e.dma_start(out=xw[s][:, :, co:co + hi - lo], in_=fm(srcd, lo, hi - lo))],
                   w=xk, dma="xw%d" % s)
            m = mi(0, 1, 1 if isctx else 0)
            emit_prenorm(xw[s], xk, W, m, h, hk, sq, rs, tmp)
            yield
            if not isctx:
                for fc in range(KC):
                    bk = xbanks[xb_i[0] % 3]
                    xb_i[0] += 1

                    def mm(e, fc=fc, bk=bk):
                        for k in range(KC):
                            ins = e.matmul(ps[:, bk, 0:T], lhsT=winb[:, k, fc * P:(fc + 1) * P], rhs=h[:, k, 2:2 + T],
                                           start=(k == 0), stop=(k == KC - 1))
                        return ins
                    S_.add("pe", mm, r=hk + wk, w=[("psX", bk)])
                    S_.add("act", lambda e, fc=fc, bk=bk: e.activation(out=gbo[s][:, fc, :], in_=ps[:, bk, 0:T],
                                                                       func=AF.Gelu_apprx_tanh),
                           r=[("psX", bk)], w=[("gbo", s, fc)])
                    if fc % 4 == 3:
                        yield
                S_.add("sp", lambda e: [e.dma_start(out=fm(GB, t0, T), in_=gbo[s])],
                       r=[("gbo", s, fc) for fc in range(KC)], w=[("GB", i)], dma="gbo%d" % s)
            for fc in range(KC):
                bk = xbanks[xb_i[0] % 3]
                xb_i[0] += 1

                def mm2(e, fc=fc, bk=bk):
                    for k in range(KC):
                        ins = e.matmul(ps[:, bk, 0:W], lhsT=winb[:, k, D + fc * P:D + (fc + 1) * P], rhs=h[:, k, 0:W],
                                       start=(k == 0), stop=(k == KC - 1))
                    return ins
                S_.add("pe", mm2, r=hk + wk, w=[("psX", bk)])
                S_.add("act", lambda e, fc=fc, bk=bk: e.activation(out=zc[:, fc, :], in_=ps[:, bk, 0:W], func=AF.Identity),
                       r=[("psX", bk)], w=[("zc", fc)])
                if fc % 4 == 3:
                    yield
            zk = [("zc", fc) for fc in range(KC)]
            if co > 0:
                S_.add("dve", lambda e: e.memset(zc[:, :, 0:co], 0.0), r=zk, w=zk)
            if hi - lo + co < W:
                S_.add("dve", lambda e: e.memset(zc[:, :, hi - lo + co:W], 0.0), r=zk, w=zk)
            xc = xcs[s]
            for fc in range(KC):
                S_.add("dve", lambda e, fc=fc: e.tensor_scalar(out=xc[:, fc, :], in0=zc[:, fc, 0:T],
                                                               scalar1=vcol("convw", fc), scalar2=vcol("convb", fc),
                                                               op0=ALU.mult, op1=ALU.add),
                       r=[("zc", fc)], w=[("xc", s, fc)])
                for j in range(1, 4):
                    S_.add("dve", lambda e, fc=fc, j=j: e.scalar_tensor_tensor(
                        out=xc[:, fc, :], in0=zc[:, fc, j:j + T], scalar=vcol("convw", j * 8 + fc), in1=xc[:, fc, :],
                        op0=ALU.mult, op1=ALU.add), r=[("zc", fc), ("xc", s, fc)], w=[("xc", s, fc)])
                S_.add("pool", lambda e, fc=fc: e.tensor_copy(out=xcbs[s][:, fc, :], in_=xc[:, fc, :]),
                       r=[("xc", s, fc)], w=[("xcb", s, fc)])
                if fc % 2 == 1:
                    yield
            if not isctx:
                S_.add("sp", lambda e: [e.dma_start(out=fm(XC, t0, T), in_=xc)],
                       r=[("xc", s, fc) for fc in range(KC)], w=[("XC", i)], dma="xcs%d" % s)

        def stageB(n):
            kind, i = tiles[n]
            s = n % 2
            isctx = kind == "ctx"
            t0 = i * T
            xc = xcs[s]
            xcb = xcbs[s]
            for gs in range(2):
                heads = list(range(gs * 4, gs * 4 + 4))
                res = emit_gates_group(0, heads, xcb, xc, tsg[gs], gs, wab, wib, sk=s)
                yield
                for (a_t, b_t, ak, bk_), hd in zip(res, heads):
                    if n == 0:
                        init, ik = 0.0, []
                    else:
                        init, ik = hfo[1 - s][:, hd, T - 1:T], [("hfo", 1 - s, hd)]
                    S_.add("dve", lambda e, hd=hd, a_t=a_t, b_t=b_t, init=init: e.tensor_tensor_scan(
                        out=hfo[s][:, hd, :], data0=a_t, data1=b_t, initial=init, op0=ALU.mult, op1=ALU.add),
                        r=[ak, bk_] + ik, w=[("hfo", s, hd)])
                yield
            if isctx:
                for gs in range(2):
                    heads = list(range(gs * 4, gs * 4 + 4))
                    res = emit_gates_group(1, heads, xcb, xc, tsg[gs], gs, wab, wib, sk=s)
                    yield
                    for (a_t, b_t, ak, bk_), hd in zip(res, heads):
                        S_.add("dve", lambda e, a_t=a_t, b_t=b_t: e.tensor_tensor_scan(
                            out=hbt[:, ::-1], data0=a_t[:, ::-1], data1=b_t[:, ::-1], initial=0.0, op0=ALU.mult, op1=ALU.add),
                            r=[ak, bk_], w=["hbt"])
                        S_.add("dve", lambda e, hd=hd: e.tensor_copy(out=st_b[:, hd:hd + 1], in_=hbt[:, 0:1]),
                               r=["hbt"], w=[("st_b", hd)])
                    yield
            if not isctx:
                S_.add("sp", lambda e: [e.dma_start(out=fm(HF, t0, T), in_=hfo[s])],
                       r=[("hfo", s, hd) for hd in range(KC)], w=[("HF", i)], dma="hfo%d" % s)

        ntl = len(tiles)
        interleave(stageA(0))
        for n in range(ntl):
            interleave(stageB(n), stageA(n + 1) if n + 1 < ntl else None)
        S_.barrier()
        ar.release(m0)

    def lru_out_phase():
        m0 = ar.mark()
        wob = ar.alloc([KC, D], BF16)
        wab = ar.alloc([2, 8, P], BF16)
        wib = ar.alloc([2, 8, P], BF16)
        xcs = [ar.alloc([KC, T], F32) for _ in range(2)]
        hfs = [ar.alloc([KC, T], F32) for _ in range(2)]
        gbs = [ar.alloc([KC, T], F32) for _ in range(2)]
        x1s = [ar.alloc([KC, T], F32) for _ in range(2)]
        xcbs = [ar.alloc([KC, T], BF16) for _ in range(2)]
        mbs = [ar.alloc([KC, T], BF16) for _ in range(2)]
        ybuf = ar.alloc([KC, T], F32)
        sq = ar.alloc([KC, T], BF16)
        rs = ar.alloc([T], F32)
        tmp = [ar.alloc([T], F32) for _ in range(2)]
        tsg = [[[ar.alloc([T], F32) for _ in range(5)] for _ in range(4)] for _ in range(2)]
        hbs = [ar.alloc([KC, T], F32) for _ in range(2)]
        smt = [ar.alloc([T], F32) for _ in range(2)]
        for k in range(KC):
            S_.add("pool", lambda e, k=k: [e.dma_start(out=wob[:, k, :], in_=lwo_d[k * P:(k + 1) * P, :])],
                   w=[("wo", k)], dma="wo%d" % k)
        load_gate_w(wab, wib)
        m = mi(0, 1, 0)
        order = list(range(NT - 1, -1, -1))

        def stageA(n):
            i = order[n]
            s = n % 2
            t0 = i * T
            for nm, dr_, bufs in (("xc", XC, xcs), ("hf", HF, hfs), ("gb", GB, gbs), ("x1", XA, x1s)):
                S_.add("sp", lambda e, dr_=dr_, bufs=bufs: [e.dma_start(out=bufs[s], in_=fm(dr_, t0, T))],
                       w=[(nm, s, c) for c in range(KC)], dma="%s%d" % (nm, s))
            for c in range(KC):
                S_.add("pool", lambda e, c=c: e.tensor_copy(out=xcbs[s][:, c, :], in_=xcs[s][:, c, :]),
                       r=[("xc", s, c)], w=[("xcb", s, c)])
            yield

        def stageB(n):
            s = n % 2
            for gs in range(2):
                heads = list(range(gs * 4, gs * 4 + 4))
                res = emit_gates_group(1, heads, xcbs[s], xcs[s], tsg[gs], gs, wab, wib, sk=s)
                yield
                for (a_t, b_t, ak, bk_), hd in zip(res, heads):
                    par = hd % 2
                    if n == 0:
                        init, ik = st_b[:, hd:hd + 1], [("st_b", hd)]
                    else:
                        init, ik = hbs[1 - s][:, hd, 0:1], [("hbs", 1 - s, hd)]
                    S_.add("dve", lambda e, hd=hd, a_t=a_t, b_t=b_t, init=init: e.tensor_tensor_scan(
                        out=hbs[s][:, hd, ::-1], data0=a_t[:, ::-1], data1=b_t[:, ::-1], initial=init,
                        op0=ALU.mult, op1=ALU.add), r=[ak, bk_] + ik, w=[("hbs", s, hd)])
                    S_.add("dve", lambda e, hd=hd, par=par: e.tensor_tensor(out=smt[par], in0=hbs[s][:, hd, :],
                                                                          in1=hfs[s][:, hd, :], op=ALU.add),
                           r=[("hbs", s, hd), ("hf", s, hd)], w=[("smt", par)])
                    S_.add("pool", lambda e, hd=hd, par=par: e.tensor_tensor(out=mbs[s][:, hd, :], in0=smt[par],
                                                                           in1=gbs[s][:, hd, :], op=ALU.mult),
                           r=[("smt", par), ("gb", s, hd)], w=[("mb", s, hd)])
                    if hd % 2 == 1:
                        yield

        def stageC(n):
            i = order[n]
            s = n % 2
            t0 = i * T
            mb = mbs[s]

            def mmo(e, d, out):
                for k in range(KC):
                    ins = e.matmul(out, lhsT=wob[:, k, d * P:(d + 1) * P], rhs=mb[:, k, :], start=(k == 0), stop=(k == KC - 1))
                return ins
            xk = [("x1", s, c) for c in range(KC)]
            emit_out_post(mmo, [("mb", s, k) for k in range(KC)] + [("wo", k) for k in range(KC)], m, x1s[s], xk, ybuf, sq, rs, tmp)
            S_.add("sp", lambda e: [e.dma_start(out=fm(XB, t0, T), in_=x1s[s])], r=xk, w=[("XB", i)], dma="x1%d" % s)
            yield

        interleave(stageA(0))
        interleave(stageB(0))
        if NT > 1:
            interleave(stageA(1))
        for n in range(NT):
            interleave(stageC(n), stageB(n + 1) if n + 1 < NT else None)
            if n + 2 < NT:
                interleave(stageA(n + 2))
        S_.barrier()
        ar.release(m0)

    def sgu_phase():
        m0 = ar.mark()
        NV = 2 * D
        swb = ar.alloc([KC, 4 * D], BF16)
        swob = ar.alloc([16, D], BF16)
        wsT = ar.alloc([8, P], BF16)
        bvb = ar.alloc([NV], F32)
        Bm = ar.alloc([16, P], F32)
        xs = [ar.alloc([KC, T], F32) for _ in range(2)]
        h = ar.alloc([KC, T], BF16)
        u = ar.alloc([16, T], BF16)
        vt = ar.alloc([NV], F32)
        vn = ar.alloc([NV], BF16)
        mbufs = [ar.alloc([16, T], BF16) for _ in range(2)]
        ybuf = ar.alloc([KC, T], F32)
        sq = ar.alloc([KC, T], BF16)
        rs = ar.alloc([T], F32)
        tmp = [ar.alloc([T], F32) for _ in range(2)]
        sq2 = ar.alloc([KC, T], BF16)
        rs2 = ar.alloc([T], F32)
        tmp2 = [ar.alloc([T], F32) for _ in range(2)]
        sm4 = [ar.alloc([4, P], F32) for _ in range(2)]
        s1 = ar.alloc([4], F32)
        s2 = ar.alloc([4], F32)
        sc_ = ar.alloc([8], F32)
        ws_f = vt[:, 0:1024].rearrange("p (g q) -> p g q", g=8)
        bsb = vt[:, 1024:2048].rearrange("p (g q) -> p g q", g=8)
        for k in range(KC):
            S_.add("pool", lambda e, k=k: [e.dma_start(out=swb[:, k, hh * D:(hh + 1) * D],
                                                       in_=swin_d[k * P:(k + 1) * P, hh * D:(hh + 1) * D]) for hh in range(4)],
                   w=[("swin", k)], dma="swin%d" % k, ndma=4)
        for hh in range(2):
            S_.add("pool", lambda e, hh=hh: [e.dma_start(out=swob[:, ch, :], in_=swo_d[ch * P:(ch + 1) * P, :])
                                             for ch in range(hh * 8, hh * 8 + 8)], w=[("swo", hh)], dma="swo%d" % hh, ndma=8)
        S_.add("sp", lambda e: [e.dma_start(out=ws_f, in_=sws_d.rearrange("g q p -> q g p"))], w=["ws_f"], dma="ws_f")
        S_.add("sp", lambda e: [e.dma_start(out=bsb, in_=sbs_d.rearrange("g q -> (g q)").partition_broadcast(P))],
               w=["bsb"], dma="bsb")
        S_.add("sp", lambda e: [e.dma_start(out=bvb, in_=sbv_d[0, :].partition_broadcast(P))], w=["bvb"], dma="bvb")
        for g2 in range(2):
            def tr(e, g2=g2):
                for gg in range(4):
                    g = g2 * 4 + gg
                    ins = e.transpose(out=ps[:, 7, gg * P:(gg + 1) * P], in_=ws_f[:, g, :], identity=ident)
                return ins
            S_.add("pe", tr, r=["ws_f", "ident"], w=["ps7"])
            S_.add("dve", lambda e, g2=g2: e.tensor_copy(out=wsT[:, g2 * 4:(g2 + 1) * 4, :].rearrange("p g q -> p (g q)"),
                                                         in_=ps[:, 7, :]), r=["ps7"], w=[("wsT", g2)])
        for g2 in range(2):
            def wsm(e, g2=g2):
                for gg in range(4):
                    ins = e.matmul(ps[:, 6, gg * P:(gg + 1) * P], lhsT=ones_bf, rhs=wsT[:, g2 * 4 + gg, :], start=True, stop=True)
                return ins
            S_.add("pe", wsm, r=[("wsT", g2), "ones"], w=["ps6"])
            for gg in range(4):
                g = g2 * 4 + gg
                for cc in range(2):
                    ch = g * 2 + cc
                    S_.add("dve", lambda e, g=g, gg=gg, ch=ch: e.scalar_tensor_tensor(
                        out=Bm[:, ch, :], in0=ps[:, 6, gg * P:(gg + 1) * P], scalar=vcol("lnb", ch), in1=bsb[:, g, :],
                        op0=ALU.mult, op1=ALU.add), r=["ps6", "bsb"], w=[("Bm", ch)])
        S_.barrier()
        m = mi(1, 1, 0)
        swk = [("swin", k) for k in range(KC)]
        hk = [("h", c) for c in range(KC)]

        def stageA(i):
            s = i % 2
            t0 = i * T
            mbuf = mbufs[s]
            xk = [("xs", s, c) for c in range(KC)]
            S_.add("sp", lambda e: [e.dma_start(out=xs[s], in_=fm(XB, t0, T))], w=xk, dma="xs%d" % s)
            emit_prenorm(xs[s], xk, T, m, h, hk, sq, rs, tmp, kt="A")
            yield

            def vmm(tc):
                for fg in range(4):
                    def mmv(e, fg=fg):
                        for k in range(KC):
                            ins = e.matmul(ps[:, 2 + fg % 2, :], lhsT=h[:, k, tc * P:(tc + 1) * P],
                                           rhs=swb[:, k, NV + fg * 512:NV + (fg + 1) * 512], start=(k == 0), stop=(k == KC - 1))
                        return ins
                    S_.add("pe", mmv, r=hk + swk, w=[("psV", fg % 2)])
                    S_.add("dve", lambda e, fg=fg: e.tensor_tensor(out=vt[:, fg * 512:(fg + 1) * 512], in0=ps[:, 2 + fg % 2, :],
                                                                   in1=bvb[:, fg * 512:(fg + 1) * 512], op=ALU.add),
                           r=[("psV", fg % 2), "bvb"], w=[("vt", fg)])

            def vpost(tc):
                for fg in range(4):
                    S_.add("act", lambda e, fg=fg: e.activation(out=vt[:, fg * 512:(fg + 1) * 512],
                                                                in_=vt[:, fg * 512:(fg + 1) * 512], func=AF.Gelu_apprx_tanh,
                                                                accum_out=s1[:, fg:fg + 1]),
                           r=[("vt", fg)], w=[("vt", fg), ("s1", fg)])
                for fg in range(4):
                    S_.add("act", lambda e, fg=fg: e.activation(out=vn[:, fg * 512:(fg + 1) * 512],
                                                                in_=vt[:, fg * 512:(fg + 1) * 512], func=AF.Square,
                                                                accum_out=s2[:, fg:fg + 1]),
                           r=[("vt", fg)], w=[("vn", fg), ("s2", fg)])
                s1k = [("s1", fg) for fg in range(4)]
                s2k = [("s2", fg) for fg in range(4)]
                S_.add("dve", lambda e: e.reduce_sum(out=sc_[:, 5:6], in_=s1, axis=mybir.AxisListType.X), r=s1k, w=["sc5"])
                S_.add("dve", lambda e: e.reduce_sum(out=sc_[:, 6:7], in_=s2, axis=mybir.AxisListType.X), r=s2k, w=["sc6"])
                S_.add("dve", lambda e: e.tensor_scalar_mul(out=sc_[:, 0:1], in0=sc_[:, 5:6], scalar1=1.0 / NV), r=["sc5"], w=["sc0"])
                S_.add("dve", lambda e: e.tensor_tensor(out=sc_[:, 1:2], in0=sc_[:, 0:1], in1=sc_[:, 0:1], op=ALU.mult),
                       r=["sc0"], w=["sc1"])
                S_.add("dve", lambda e: e.scalar_tensor_tensor(out=sc_[:, 2:3], in0=sc_[:, 6:7], scalar=1.0 / NV, in1=sc_[:, 1:2],
                                                               op0=ALU.mult, op1=ALU.subtract), r=["sc6", "sc1"], w=["sc2"])
                S_.add("act", lambda e: e.activation(out=sc_[:, 2:3], in_=sc_[:, 2:3], func=AF.Sqrt, bias=epsc[:, 0:1], scale=1.0),
                       r=["sc2", "epsc"], w=["sc2"])
                S_.add("dve", lambda e: e.reciprocal(out=sc_[:, 3:4], in_=sc_[:, 2:3]), r=["sc2"], w=["sc3"])
                S_.add("dve", lambda e: e.scalar_tensor_tensor(out=sc_[:, 4:5], in0=sc_[:, 0:1], scalar=-1.0, in1=sc_[:, 3:4],
                                                               op0=ALU.mult, op1=ALU.mult), r=["sc0", "sc3"], w=["sc4"])
                vtk = [("vt", fg) for fg in range(4)]
                vnk = [("vn", fg) for fg in range(4)]
                S_.add("act", lambda e: e.activation(out=vn, in_=vt, func=AF.Identity, bias=sc_[:, 4:5], scale=sc_[:, 3:4]),
                       r=vtk + ["sc3", "sc4"], w=vnk)

            def smm(tc):
                vnk = [("vn", fg) for fg in range(4)]
                for c4 in range(4):
                    par = c4 % 2
                    bk = (5, 7)[par]

                    def mms(e, c4=c4, bk=bk):
                        for cc in range(4):
                            ch = c4 * 4 + cc
                            ins = e.matmul(ps[:, bk, cc * P:(cc + 1) * P], lhsT=vn[:, ch * P:(ch + 1) * P],
                                           rhs=wsT[:, ch // 2, :], start=True, stop=True)
                        return ins
                    S_.add("pe", mms, r=vnk + [("wsT", 0), ("wsT", 1)], w=[("psS", par)])
                    for cc in range(4):
                        ch = c4 * 4 + cc
                        S_.add("dve", lambda e, ch=ch, cc=cc, par=par, bk=bk: e.scalar_tensor_tensor(
                            out=sm4[par][:, cc, :], in0=ps[:, bk, cc * P:(cc + 1) * P], scalar=vcol("lng", ch),
                            in1=Bm[:, ch, :], op0=ALU.mult, op1=ALU.add),
                            r=[("psS", par), ("Bm", ch)], w=[("sm4", par, cc)])
                    S_.add("pool", lambda e, c4=c4, par=par: e.tensor_tensor(
                        out=mbuf[:, c4 * 4:(c4 + 1) * 4, tc * P:(tc + 1) * P], in0=sm4[par],
                        in1=u[:, c4 * 4:(c4 + 1) * 4, tc * P:(tc + 1) * P], op=ALU.mult),
                        r=[("sm4", par, cc) for cc in range(4)] + [("u", c4 * 4 + cc) for cc in range(4)],
                        w=[("m", s, c4, tc)])

            vmm(0)
            yield
            for fc in range(16):
                def mm(e, fc=fc):
                    for k in range(KC):
                        ins = e.matmul(ps[:, fc % 2, 0:T], lhsT=swb[:, k, fc * P:(fc + 1) * P], rhs=h[:, k, :],
                                       start=(k == 0), stop=(k == KC - 1))
                    return ins
                S_.add("pe", mm, r=hk + swk, w=[("psA", fc % 2)])
                S_.add("act", lambda e, fc=fc: e.activation(out=u[:, fc, :], in_=ps[:, fc % 2, 0:T], func=AF.Gelu_apprx_tanh,
                                                            bias=vcol("sbin", fc), scale=1.0),
                       r=[("psA", fc % 2)], w=[("u", fc)])
                if fc == 7:
                    vpost(0)
                if fc % 4 == 3:
                    yield
            smm(0)
            yield
            vmm(1)
            yield
            vpost(1)
            yield
            smm(1)

        def stageB(i):
            s = i % 2
            t0 = i * T
            mbuf = mbufs[s]
            xk = [("xs", s, c) for c in range(KC)]

            def mmo(e, d, out):
                for ch in range(16):
                    ins = e.matmul(out, lhsT=swob[:, ch, d * P:(d + 1) * P], rhs=mbuf[:, ch, :], start=(ch == 0), stop=(ch == 15))
                return ins
            emit_out_post(mmo, [("m", s, c4, tc) for c4 in range(4) for tc in range(2)] + [("swo", 0), ("swo", 1)],
                          m, xs[s], xk, ybuf, sq2, rs2, tmp2, kt="B", ybanks=(4, 4))
            S_.add("sp", lambda e: [e.dma_start(out=fm(XA, t0, T), in_=xs[s])], r=xk, w=[("XA", i)], dma="xs%d" % s)
            yield

        interleave(stageA(0))
        for i in range(NT):
            interleave(stageA(i + 1) if i + 1 < NT else None, stageB(i))
        S_.barrier()
        ar.release(m0)

    dbgs = []

    def dump(n, srcd):
        if dbg:
            dd = nc.dram_tensor("dbg%d" % n, [KC, P, S], F32, kind="ExternalOutput").ap()
            S_.add("sp", lambda e: [e.dma_start(out=dd[:, :, :], in_=srcd[:, :, :])], w=[("DBG", n)], dma="dbg%d" % n)
            S_.barrier()

    phase0()
    if only == 'sgu':
        sgu_phase()
        dump(5, XA)
        nph = 0
    if nph >= 1:
        ffn_phase(0, 0, 0, None, XA, do_ctx=True, first=True)
        dump(1, XA)
    if nph >= 2:
        lru_in_phase()
    if nph >= 3:
        lru_out_phase()
        dump(2, XB)
    if nph >= 4:
        ffn_phase(0, 1, 2, XB, XA, barrier_after=dbg or nph < 5)
        dump(3, XA)
    if nph >= 5:
        ffn_phase(1, 0, 0, XA, XB)
        dump(4, XB)
    if nph >= 6:
        sgu_phase()
        dump(5, XA)
    if nph >= 7:
        ffn_phase(1, 1, 2, XA, None, last=True)
    S_.emit(nc, stack)
    stack.close()
    return nc


def make_in_maps(inp, S=SEQ):
    f = lambda a: np.ascontiguousarray(np.asarray(a, dtype=np.float32))
    B = inp["x"].shape[0]
    shared_rows = [
        f(inp["ada_b"]).reshape(144, P), f(inp["norm_pre"]).reshape(48, P), f(inp["norm_post"]).reshape(48, P),
        f(inp["lru_conv_w"]).reshape(32, P), f(inp["lru_conv_b"]).reshape(8, P), f(inp["lru_ba"]).reshape(16, P),
        f(inp["lru_bi"]).reshape(16, P), f(inp["lru_lambda"]).reshape(16, P),
        f(inp["sgu_b_in"])[0, :2 * D].reshape(16, P), f(inp["sgu_ln_g"]).reshape(16, P), f(inp["sgu_ln_b"]).reshape(16, P),
    ]
    shared = {
        "ada_w": f(inp["ada_w"]), "ffn_w1": f(inp["ffn_w1"]), "ffn_w3": f(inp["ffn_w3"]), "ffn_w2": f(inp["ffn_w2"]),
        "lru_w_in": f(inp["lru_w_in"])[0], "lru_wa": f(inp["lru_wa"])[0], "lru_wi": f(inp["lru_wi"])[0],
        "lru_w_out": f(inp["lru_w_out"])[0], "sgu_w_in": f(inp["sgu_w_in"])[0],
        "sgu_b_in_v": f(inp["sgu_b_in"])[:, 2 * D:], "sgu_ws": f(inp["sgu_ws"])[0], "sgu_bs": f(inp["sgu_bs"])[0],
        "sgu_w_out": f(inp["sgu_w_out"])[0],
    }
    maps = []
    for b in range(B):
        vecs = np.zeros((NROWS_PAD, P), np.float32)
        vecs[:NROWS] = np.concatenate(shared_rows + [f(inp["c"])[b].reshape(8, P), f(inp["c_ctx"]).reshape(8, P)], axis=0)
        m = dict(shared)
        m["x"] = f(inp["x"][b, :S])
        m["ctx"] = f(inp["ctx"][b])
        m["vecs"] = vecs
        maps.append(m)
    return maps


_NC_CACHE = {}


def kernel(**inputs):
    maps = make_in_maps(inputs)
    if "nc" not in _NC_CACHE:
        _NC_CACHE["nc"] = build()
    res = run_bass_kernel_spmd(_NC_CACHE["nc"], maps, core_ids=list(range(NCORES)))
    return np.stack([np.asarray(r["out"], dtype=np.float32) for r in res.results], axis=0)
```
